# Optimizing a Trainium2 kernel written in Bass

```python
import math
import jax, jax.numpy as jnp
from jax import lax
import numpy as np

D_MODEL = 2048
BATCH = 16
SEQ = 2048
DEPTH = 1
DEC_BATCH = 128
DEC_SEQ = 1
PAST_LEN = 16384
PAGE_SIZE = 128

N_META = 16
HEAD_DIM = 64
ATTN_WIDTH = D_MODEL // 2
N_HEADS = ATTN_WIDTH // HEAD_DIM
N_KV_HEADS = N_HEADS // 4
GQA_GROUP = N_HEADS // N_KV_HEADS
WINDOW = 128
BLOCK = 128
ROT_DIM = HEAD_DIM // 4
ROPE_THETA = 500000.0
POOL_WIDTH = D_MODEL - ATTN_WIDTH
POOL_WINDOWS = (2, 4, 8, 16)
N_POOL_GROUPS = len(POOL_WINDOWS)
POOL_GROUP_WIDTH = POOL_WIDTH // N_POOL_GROUPS
POOL_HIST = max(POOL_WINDOWS) - 1
Q_COLS = N_HEADS * HEAD_DIM
KV_COLS = N_KV_HEADS * HEAD_DIM
IN_COLS = Q_COLS + 2 * KV_COLS + POOL_WIDTH
D_FF = ((-(-8 * D_MODEL // 3)) + 255) // 256 * 256
DEEPNORM_ALPHA = (2.0 * DEPTH) ** 0.25
DEEPNORM_BETA = (8.0 * DEPTH) ** -0.25
LN_EPS = 1e-5
NEG_INF = -1e30

kernel_name = "hymba_swa_sink_multiscale_pool_deepnorm_step"


def layer_norm(x, g, b):
    xf = x.astype(jnp.float32)
    mu = jnp.mean(xf, -1, keepdims=True)
    var = jnp.mean(jnp.square(xf - mu), -1, keepdims=True)
    return ((xf - mu) * lax.rsqrt(var + LN_EPS) * g.astype(jnp.float32) + b.astype(jnp.float32)).astype(x.dtype)


def partial_rope(x, pos):
    half = ROT_DIM // 2
    inv_freq = ROPE_THETA ** (-jnp.arange(half, dtype=jnp.float32) * 2.0 / ROT_DIM)
    ang = pos.astype(jnp.float32)[:, None] * inv_freq
    shape = (pos.shape[0],) + (1,) * (x.ndim - 3) + (half,)
    cos = jnp.cos(ang).reshape(shape)
    sin = jnp.sin(ang).reshape(shape)
    x1 = x[..., :half].astype(jnp.float32)
    x2 = x[..., half:ROT_DIM].astype(jnp.float32)
    rot = jnp.concatenate([x1 * cos - x2 * sin, x2 * cos + x1 * sin], -1).astype(x.dtype)
    return jnp.concatenate([rot, x[..., ROT_DIM:]], -1)


def mixer_inputs(h, pos, w_in, b_in):
    bn, t, _ = h.shape
    z = jnp.einsum('btd,de->bte', h, w_in) + b_in
    q = z[..., :Q_COLS].reshape(bn, t, N_KV_HEADS, GQA_GROUP, HEAD_DIM)
    k = z[..., Q_COLS:Q_COLS + KV_COLS].reshape(bn, t, N_KV_HEADS, HEAD_DIM)
    v = z[..., Q_COLS + KV_COLS:Q_COLS + 2 * KV_COLS].reshape(bn, t, N_KV_HEADS, HEAD_DIM)
    u = z[..., Q_COLS + 2 * KV_COLS:]
    return partial_rope(q, pos), partial_rope(k, pos), v, u


def sink_attention(q, k, v, mask, sinks):
    s = jnp.einsum('...qhgd,...shd->...hgqs', q, k, preferred_element_type=jnp.float32) * (1.0 / math.sqrt(HEAD_DIM))
    s = jnp.where(mask, s, NEG_INF)
    sink = sinks.astype(jnp.float32)[:, :, None, None]
    m = jnp.maximum(jnp.max(s, -1, keepdims=True), sink)
    p = jnp.exp(s - m)
    denom = jnp.sum(p, -1, keepdims=True) + jnp.exp(sink - m)
    return jnp.einsum('...hgqs,...shd->...qhgd', (p / denom).astype(v.dtype), v)


def prompt_window_attention(q, k, v, sinks):
    bn, L = q.shape[:2]
    pad = (-L) % BLOCK
    lp = L + pad
    nb = lp // BLOCK
    padt = lambda a: jnp.pad(a, ((0, 0), (pad, 0)) + ((0, 0),) * (a.ndim - 2))
    qb = padt(q).reshape(bn, nb, BLOCK, N_KV_HEADS, GQA_GROUP, HEAD_DIM)
    kb = padt(k).reshape(bn, nb, BLOCK, N_KV_HEADS, HEAD_DIM)
    vb = padt(v).reshape(bn, nb, BLOCK, N_KV_HEADS, HEAD_DIM)
    shift = lambda a: jnp.concatenate([jnp.zeros_like(a[:, :1]), a[:, :-1]], 1)
    kband = jnp.concatenate([shift(kb), kb], 2)
    vband = jnp.concatenate([shift(vb), vb], 2)
    qpos = (jnp.arange(lp, dtype=jnp.int32) - pad).reshape(nb, BLOCK)
    kpos = jnp.concatenate([qpos - BLOCK, qpos], 1)
    diff = qpos[:, :, None] - kpos[:, None, :]
    mask = (diff >= 0) & (diff <= WINDOW) & (kpos[:, None, :] >= 0)
    o = sink_attention(qb, kband, vband, mask[:, None, None], sinks)
    return o.reshape(bn, lp, Q_COLS)[:, pad:]


def sample_window_attention(q, k_all, v_all, pos_new, sinks):
    bn, t = q.shape[:2]
    s = k_all.shape[1]
    kpos = pos_new[0] - (s - t) + jnp.arange(s, dtype=jnp.int32)
    diff = pos_new[:, None] - kpos[None, :]
    mask = (diff >= 0) & (diff <= WINDOW)
    return sink_attention(q, k_all, v_all, mask, sinks).reshape(bn, t, Q_COLS)


def pool_mixer(u_hist, u_new, pos_new, w_pool, pool_scale):
    bn, t, _ = u_new.shape
    ext = jnp.concatenate([u_hist, u_new], 1)
    extf = ext.astype(jnp.float32)
    csum = jnp.cumsum(jnp.pad(extf, ((0, 0), (1, 0), (0, 0))), axis=1)
    hi = csum[:, POOL_HIST + 1:]
    cur = extf[:, POOL_HIST:]
    outs = []
    for gi, w in enumerate(POOL_WINDOWS):
        sl = slice(gi * POOL_GROUP_WIDTH, (gi + 1) * POOL_GROUP_WIDTH)
        lo = csum[:, POOL_HIST + 1 - w:POOL_HIST + 1 - w + t, sl]
        cnt = jnp.minimum(w, pos_new + 1).astype(jnp.float32)[:, None]
        outs.append((hi[..., sl] - lo) / cnt - cur[..., sl])
    d = jnp.stack(outs, 2).astype(u_new.dtype)
    y = jnp.einsum('btgc,gcd->btgd', d, w_pool).reshape(bn, t, POOL_WIDTH) * pool_scale
    return y, ext[:, -POOL_HIST:]


def post_sublayers(h, attn_out, pool_out, w_o, ln1_g, ln1_b, w_gate, w_up, w_down, ln2_g, ln2_b):
    mix = jnp.einsum('bte,ed->btd', jnp.concatenate([attn_out, pool_out], -1), w_o)
    h1 = layer_norm(DEEPNORM_ALPHA * h + mix, ln1_g, ln1_b)
    a = jax.nn.silu(jnp.einsum('btd,df->btf', h1, w_gate)) * jnp.einsum('btd,df->btf', h1, w_up)
    ff = jnp.einsum('btf,fd->btd', a, w_down)
    return layer_norm(DEEPNORM_ALPHA * h1 + ff, ln2_g, ln2_b)


def setup_inputs(seed: int = 0) -> dict:
    key = jax.random.key(seed)
    ks = jax.random.split(key, 24)
    f32 = jnp.float32
    nrm = lambda k, shape, s: jax.random.normal(k, shape, f32) * s
    w_keep = min(WINDOW, PAST_LEN)
    return {
        "x_prompt": nrm(ks[0], (BATCH, SEQ, D_MODEL), 1.0),
        "x_sample": nrm(ks[1], (DEC_BATCH, DEC_SEQ, D_MODEL), 1.0),
        "cache_k": nrm(ks[2], (DEPTH, DEC_BATCH, w_keep, N_KV_HEADS, HEAD_DIM), 1.0),
        "cache_v": nrm(ks[3], (DEPTH, DEC_BATCH, w_keep, N_KV_HEADS, HEAD_DIM), 1.0),
        "state_pool": nrm(ks[4], (DEPTH, DEC_BATCH, POOL_HIST, POOL_WIDTH), 1.0),
        "meta_tokens": nrm(ks[5], (N_META, D_MODEL), 1.0),
        "ln_in_g": 1.0 + nrm(ks[6], (D_MODEL,), 0.02),
        "ln_in_b": nrm(ks[7], (D_MODEL,), 0.02),
        "w_in": nrm(ks[8], (DEPTH, D_MODEL, IN_COLS), D_MODEL ** -0.5),
        "b_in": nrm(ks[9], (DEPTH, IN_COLS), 0.02),
        "attn_sinks": nrm(ks[10], (DEPTH, N_HEADS), 1.0),
        "w_pool": nrm(ks[11], (DEPTH, N_POOL_GROUPS, POOL_GROUP_WIDTH, POOL_GROUP_WIDTH), POOL_GROUP_WIDTH ** -0.5),
        "pool_scale": 1.0 + nrm(ks[12], (DEPTH, POOL_WIDTH), 0.1),
        "w_o": nrm(ks[13], (DEPTH, D_MODEL, D_MODEL), D_MODEL ** -0.5 * DEEPNORM_BETA),
        "ln1_g": 1.0 + nrm(ks[14], (DEPTH, D_MODEL), 0.02),
        "ln1_b": nrm(ks[15], (DEPTH, D_MODEL), 0.02),
        "w_gate": nrm(ks[16], (DEPTH, D_MODEL, D_FF), D_MODEL ** -0.5),
        "w_up": nrm(ks[17], (DEPTH, D_MODEL, D_FF), D_MODEL ** -0.5),
        "w_down": nrm(ks[18], (DEPTH, D_FF, D_MODEL), D_FF ** -0.5 * DEEPNORM_BETA),
        "ln2_g": 1.0 + nrm(ks[19], (DEPTH, D_MODEL), 0.02),
        "ln2_b": nrm(ks[20], (DEPTH, D_MODEL), 0.02),
    }


def reference(x_prompt, x_sample, cache_k, cache_v, state_pool, meta_tokens, ln_in_g, ln_in_b,
              w_in, b_in, attn_sinks, w_pool, pool_scale, w_o, ln1_g, ln1_b,
              w_gate, w_up, w_down, ln2_g, ln2_b):
    bp = x_prompt.shape[0]
    t_s = x_sample.shape[1]
    w_keep = cache_k.shape[2]
    meta = jnp.broadcast_to(meta_tokens.astype(x_prompt.dtype)[None], (bp, N_META, D_MODEL))
    hp = layer_norm(jnp.concatenate([meta, x_prompt], 1), ln_in_g, ln_in_b)
    hs = layer_norm(x_sample, ln_in_g, ln_in_b)
    pos_p = jnp.arange(hp.shape[1], dtype=jnp.int32)
    pos_s = PAST_LEN + jnp.arange(t_s, dtype=jnp.int32)
    nkp, nvp, nup, nks, nvs, nus = [], [], [], [], [], []
    for l in range(DEPTH):
        sinks = attn_sinks[l].reshape(N_KV_HEADS, GQA_GROUP)
        ffn_args = (w_o[l], ln1_g[l], ln1_b[l], w_gate[l], w_up[l], w_down[l], ln2_g[l], ln2_b[l])
        qp, kp, vp, up = mixer_inputs(hp, pos_p, w_in[l], b_in[l])
        ap = prompt_window_attention(qp, kp, vp, sinks)
        pp, hist_p = pool_mixer(jnp.zeros((bp, POOL_HIST, POOL_WIDTH), up.dtype), up, pos_p, w_pool[l], pool_scale[l])
        nkp.append(kp[:, -WINDOW:]); nvp.append(vp[:, -WINDOW:]); nup.append(hist_p)
        hp = post_sublayers(hp, ap, pp, *ffn_args)
        qs, ks_, vs, us = mixer_inputs(hs, pos_s, w_in[l], b_in[l])
        k_all = jnp.concatenate([cache_k[l], ks_], 1)
        v_all = jnp.concatenate([cache_v[l], vs], 1)
        a_s = sample_window_attention(qs, k_all, v_all, pos_s, sinks)
        ps, hist_s = pool_mixer(state_pool[l], us, pos_s, w_pool[l], pool_scale[l])
        nks.append(k_all[:, -w_keep:]); nvs.append(v_all[:, -w_keep:]); nus.append(hist_s)
        hs = post_sublayers(hs, a_s, ps, *ffn_args)
    y_prompt = hp[:, N_META:]
    y_sample = hs
    return (y_prompt, y_sample, jnp.stack(nkp), jnp.stack(nvp), jnp.stack(nup), jnp.stack(nks), jnp.stack(nvs), jnp.stack(nus))
```

```python
import contextlib, os
import numpy as np
import concourse.bass as bass
import concourse.mybir as mybir
from concourse.bass_utils import run_bass_kernel_spmd

F32 = mybir.dt.float32
BF16 = mybir.dt.bfloat16
AF = mybir.ActivationFunctionType
ALU = mybir.AluOpType
AX = mybir.AxisListType

D = 2048; DFF = 5632; NKC = 16; NFC = 44; P = 128
NCORE = 8; SEQ = 2048; NSAMP = 16; NMETA = 16
ALPHA = float(2.0 ** 0.25)
EPS = 1e-5
HL = 16
R_SLOTS = 5
UNIT = 2048

def unit_table():
    u = []
    for c in range(8): u.append(("q", c, 16))
    for g in range(4): u.append(("k", g, 16))
    for h in range(2): u.append(("v", h, 8))
    for f in range(8): u.append(("u", f, 16))
    for dc in range(16): u.append(("o", dc, 16))
    for f in range(NFC):
        u.append(("g", f, 16)); u.append(("up", f, 16))
    for dc in range(16):
        u.append(("d", (dc, 0), 16)); u.append(("d", (dc, 1), 16)); u.append(("d", (dc, 2), 12))
    return u

UNITS = unit_table()
NUNIT = len(UNITS)
CONV_CHUNKS = [(0, 22), (22, 38), (38, 66), (66, 96), (96, 126), (126, 150), (150, NUNIT)]
META_UNITS = [i for i, u in enumerate(UNITS) if u[0] in ("k", "v", "u")]


def _tileS(W, c0, k0, k1, ncol=128):
    nk = k1 - k0
    a = W[k0 * 128:k1 * 128, c0:c0 + ncol].reshape(nk, 128, ncol).transpose(1, 0, 2).reshape(128, nk * ncol)
    out = np.zeros((128, UNIT), np.float32)
    out[:, :nk * ncol] = a
    return out


def host_weights(w_in, w_o, w_gate, w_up, w_down):
    wt = np.zeros((NUNIT, 128, UNIT), np.float32)
    for i, (kind, idx, nk) in enumerate(UNITS):
        if kind == "q":
            wt[i] = _tileS(w_in, idx * 128, 0, 16)
        elif kind == "k":
            cols = w_in[:, 1024 + idx * 64:1024 + idx * 64 + 64]
            wt[i] = _tileS(np.concatenate([cols, cols], 1), 0, 0, 16)
        elif kind == "v":
            wt[i] = _tileS(w_in, 1280, idx * 8, idx * 8 + 8, ncol=256)
        elif kind == "u":
            wt[i] = _tileS(w_in, 1536 + idx * 128, 0, 16)
        elif kind == "o":
            wt[i] = _tileS(w_o, idx * 128, 0, 16)
        elif kind == "g":
            wt[i] = _tileS(w_gate, idx * 128, 0, 16)
        elif kind == "up":
            wt[i] = _tileS(w_up, idx * 128, 0, 16)
        elif kind == "d":
            dc, part = idx
            k0 = part * 16
            wt[i] = _tileS(w_down, dc * 128, k0, k0 + nk)
    return wt


CV = {}
def _cv_layout():
    off = 0
    for name, n in (("bq", 8), ("bk", 4), ("bu", 8), ("ps", 8), ("g0", 16), ("b0", 16), ("g1", 16), ("b1", 16),
                    ("g2", 16), ("b2", 16)):
        CV[name] = off; off += n
    return off
NCV = _cv_layout()


def host_colvec(b_in, pool_scale, ln_in_g, ln_in_b, ln1_g, ln1_b, ln2_g, ln2_b):
    cv = np.zeros((128, NCV), np.float32)
    cv[:, CV["bq"]:CV["bq"] + 8] = b_in[0:1024].reshape(8, 128).T
    bk = b_in[1024:1280].reshape(4, 64)
    cv[:, CV["bk"]:CV["bk"] + 4] = np.concatenate([bk, bk], 1).T
    cv[:, CV["bu"]:CV["bu"] + 8] = b_in[1536:2560].reshape(8, 128).T
    cv[:, CV["ps"]:CV["ps"] + 8] = pool_scale.reshape(8, 128).T
    for nm, v in (("g0", ln_in_g), ("b0", ln_in_b), ("g1", ln1_g), ("b1", ln1_b), ("g2", ln2_g), ("b2", ln2_b)):
        cv[:, CV[nm]:CV[nm] + 16] = v.reshape(16, 128).T
    return cv


def host_consts():
    ident = np.eye(128, dtype=np.float32)
    perm = np.zeros((128, 128), np.float32)
    for m in range(128):
        d = m % 64
        if d < 8: perm[m + 8, m] = 1.0
        elif d < 16: perm[m - 8, m] = 1.0
    pi = np.arange(128)[:, None]; qi = np.arange(128)[None, :]
    prev = (pi >= qi).astype(np.float32); cur = (pi <= qi).astype(np.float32)
    reg = np.concatenate([prev, cur], 1)
    first = np.concatenate([prev * (pi >= 112), cur], 1)
    sprev = np.zeros((128, 128), np.float32); sprev[:, 0:16] = 1.0
    samp = np.concatenate([sprev, np.eye(128, dtype=np.float32)], 1)
    masks = np.stack([np.concatenate([m, m], 1) for m in (reg, first, samp)])
    inv = (np.float32(500000.0) ** (-np.arange(8, dtype=np.float32) * np.float32(2.0) / np.float32(16.0))).astype(np.float32)
    rope = np.zeros((5, 2, 128, 512), np.float32)
    rope[:, 0] = 1.0
    for t in range(5):
        if t < 4:
            pos = (16 + 512 * t + np.arange(512)).astype(np.float32)
        else:
            pos = np.zeros(512, np.float32)
            pos[0:16] = 16384.0
            pos[112:128] = np.arange(16, dtype=np.float32)
        ang = (pos[None, :] * inv[:, None]).astype(np.float32)
        c = np.cos(ang.astype(np.float64)).astype(np.float32)
        s = np.sin(ang.astype(np.float64)).astype(np.float32)
        for base in (0, 64):
            rope[t, 0, base:base + 8] = c; rope[t, 0, base + 8:base + 16] = c
            rope[t, 1, base:base + 8] = -s; rope[t, 1, base + 8:base + 16] = s
    return ident, perm, masks, rope


class Sched:
    ENGS = ("pe", "act", "dve", "pool", "sp")

    def __init__(self, nc, es):
        self.nc = nc; self.es = es
        self.q = {e: [] for e in self.ENGS}
        self.sem = {e: es.enter_context(nc.semaphore("prog_" + e)) for e in self.ENGS}
        self.cnt = {e: 0 for e in self.ENGS}
        self.known = {e: {} for e in self.ENGS}
        self.lastw = {}; self.readers = {}
        self.dcnt = {}

    def dma_sem(self, name):
        s = self.es.enter_context(self.nc.semaphore(name)); self.dcnt[id(s)] = 0
        return s

    def op(self, eng, fn, reads=(), writes=(), dsem=None):
        deps = []
        for r in reads:
            t = self.lastw.get(r)
            if t is not None: deps.append(t)
            if eng != "pe" and isinstance(r, tuple) and r[0] == "bank":
                deps.extend(self.readers.get(r, ()))
        for w in writes:
            t = self.lastw.get(w)
            if t is not None: deps.append(t)
            deps.extend(self.readers.get(w, ()))
        need = {}
        for (s, v) in deps:
            if need.get(id(s), (None, 0))[1] < v: need[id(s)] = (s, v)
        kn = self.known[eng]
        for k, (s, v) in need.items():
            if eng == "pe" and s is self.sem["pe"]:
                continue
            if kn.get(k, 0) < v:
                kn[k] = v
                self.q[eng].append(lambda h, s=s, v=v: h.wait_ge(s, v))
        if dsem is not None:
            self.dcnt[id(dsem)] += 16; tok = (dsem, self.dcnt[id(dsem)])
            self.q[eng].append(lambda h, fn=fn, s=dsem: fn(h).then_inc(s, 16))
        else:
            self.cnt[eng] += 1; tok = (self.sem[eng], self.cnt[eng])
            self.q[eng].append(lambda h, fn=fn, s=self.sem[eng]: fn(h).then_inc(s, 1))
        for r in reads: self.readers.setdefault(r, []).append(tok)
        for w in writes: self.lastw[w] = tok; self.readers[w] = []
        return tok

    def wait_tok(self, eng, tok):
        s, v = tok
        if self.known[eng].get(id(s), 0) < v:
            self.known[eng][id(s)] = v
            self.q[eng].append(lambda h, s=s, v=v: h.wait_ge(s, v))

    def emit(self):
        with self.nc.Block() as block:
            @block.tensor
            def _(h):
                for f in self.q["pe"]: f(h)

            @block.scalar
            def _(h):
                for f in self.q["act"]: f(h)

            @block.vector
            def _(h):
                for f in self.q["dve"]: f(h)

            @block.gpsimd
            def _(h):
                for f in self.q["pool"]: f(h)

            @block.sync
            def _(h):
                for f in self.q["sp"]: f(h)


class Builder:
    def __init__(self, n_reg_tiles=8, stage=99):
        self.n_reg_tiles = n_reg_tiles; self.stage = stage

    def build(self):
        nc = bass.Bass("TRN2", target_bir_lowering=False)
        self.nc = nc
        es = contextlib.ExitStack()
        with es:
            self.es = es
            self.S = Sched(nc, es)
            self.declare()
            self.prologue()
            self.run_tiles()
            self.epilogue()
            self.S.emit()
        return nc

    def declare(self):
        nc, es, S = self.nc, self.es, self.S
        di = lambda n, sh, dt=F32: nc.dram_tensor(n, sh, dt, kind="ExternalInput").ap()
        do = lambda n, sh: nc.dram_tensor(n, sh, F32, kind="ExternalOutput").ap()
        self.xp = di("xp", [2 * SEQ, D]); self.xs = di("xs", [128, D])
        self.ck = di("ck", [NSAMP, 128, 256]); self.cvv = di("cvv", [NSAMP, 128, 256])
        self.spool = di("spool", [NSAMP, 15, 1024])
        self.wt = di("wt", [NUNIT, 128, UNIT])
        self.wpool = di("wpool", [128, 2048])
        self.colvec = di("colvec", [128, NCV]); self.bv = di("bv", [256]); self.sinks = di("sinks", [16])
        self.c_ident = di("c_ident", [128, 128]); self.c_perm = di("c_perm", [128, 128])
        self.c_masks = di("c_masks", [3, 128, 512]); self.c_rope = di("c_rope", [5, 2, 128, 512])
        self.yp = do("yp", [2 * SEQ, D]); self.ys = do("ys", [NSAMP, D])
        self.nkp = do("nkp", [2, 128, 256]); self.nvp = do("nvp", [2, 128, 256]); self.npp = do("npp", [2, 15, 1024])
        self.nks = do("nks", [NSAMP, 128, 256]); self.nvs = do("nvs", [NSAMP, 128, 256])
        self.nps = do("nps", [NSAMP, 15, 1024])
        self.wsc = nc.dram_tensor("wsc", [NUNIT, 128, UNIT], BF16, kind="Internal").ap()
        self.dbg = do("dbg", [128, 16 * 512]) if self.stage < 99 else None

        sb = lambda n, sh, dt: es.enter_context(nc.sbuf_tensor(n, sh, dt))
        self.resT = sb("resT", [128, 16, 512], F32)
        self.hT = sb("hT", [128, 16, 512], BF16)
        self.U = sb("U", [128, NFC * 512], BF16)
        o = NFC * 128
        self.cKT = self.U[:, o:o + 8192].rearrange("p (b g k) -> p b g k", b=16, g=4); o += 8192
        self.cVe = self.U[:, o:o + 4160].rearrange("p (b g d) -> p b g d", b=16, g=4); o += 4160
        self.Pexp = self.U[:, o:o + 4096].rearrange("p (h b c) -> p h b c", h=16, b=16)
        self.histT = self.U[:, o:o + 3840].bitcast(F32).rearrange("p (f b r) -> p f b r", f=8, b=16)
        assert o + 4096 <= NFC * 512
        self.qT = sb("qT", [128, 8, 512], BF16)
        self.kT = sb("kT", [128, 4, 5 * 128], BF16)
        self.Vx = sb("Vx", [128, 5, 4, 65], BF16)
        self.zf = sb("zf", [128, 2, 512], F32); self.zb = sb("zb", [128, 2, 512], BF16)
        self.t2 = sb("t2", [128, 2, 512], F32)
        self.rope = sb("rope", [128, 2, 512], F32)
        self.pA = sb("pA", [128, 2, 512 + HL], F32)
        self.pT = sb("pT", [128, 4, 512], BF16)
        self.ao = sb("ao", [128, 2, 1024], BF16)
        self.xin = sb("xin", [128, 3, 2048], F32)
        self.rbf = sb("rbf", [128, 2, 512], BF16); self.rsq = sb("rsq", [128, 2, 512], BF16)
        self.st_mean = sb("st_mean", [128, 512], F32); self.st_rstd = sb("st_rstd", [128, 512], F32)
        self.st_tmp = sb("st_tmp", [128, 512], F32)
        self.tt = sb("tt", [128, 2, 512], F32)
        self.kf = self.st_tmp[:, :].rearrange("p (g t) -> p g t", g=4)
        self.identf = sb("identf", [128, 128], F32); self.identb = sb("identb", [128, 128], BF16)
        self.onesb = sb("onesb", [128, 128], BF16); self.permb = sb("permb", [128, 128], BF16)
        self.masks = sb("masks", [128, 3, 512], BF16)
        self.bvt = sb("bvt", [128, 256], F32)
        self.wp = sb("wp", [128, 2048], BF16)
        self.cv = sb("cv", [128, NCV], F32)
        self.sinkx = sb("sinkx", [128, 16], F32)
        self.den = sb("den", [128, 2, 4], F32)
        self.bst = sb("bst", [128, 3, 4, 6], F32); self.bmv = sb("bmv", [128, 3, 2], F32); self.brs = sb("brs", [128, 3, 2], F32)
        self.metaK = sb("metaK", [128, 4, 128], BF16); self.metaV = sb("metaV", [128, 4, 65], BF16)
        self.metaU = sb("metaU", [128, 8, 15], F32); self.uhist = sb("uhist", [128, 8, 15], F32)
        self.vst = sb("vst", [128, 2, 256], F32)
        self.ring = sb("ring", [128, R_SLOTS, UNIT], BF16)
        self.bank = [es.enter_context(nc.psum_tensor("bank%d" % i, [128, 512], F32)) for i in range(8)]

        self.ring_sem = [S.dma_sem("ring%d" % i) for i in range(R_SLOTS)]
        self.conv_sem = [S.dma_sem("conv%d" % i) for i in range(len(CONV_CHUNKS))]
        self.c_sem = S.dma_sem("consts")
        self.x_sem = [S.dma_sem("xin%d" % i) for i in range(3)]
        self.y_sem = [S.dma_sem("yout%d" % i) for i in range(3)]
        self.m_sem = S.dma_sem("misc_out"); self.v_sem = [S.dma_sem("vst0"), S.dma_sem("vst1")]
        self.rope_sem = S.dma_sem("rope")
        self.rr = {k: 0 for k in ("A", "G", "Bup", "den", "po2", "tt4", "z", "pt", "att", "po", "x", "ln", "tt", "ao", "pl")}

    XBUF = (0, 1, 2, 0)

    def x_prefetch(self, special, seq, j, b):
        S = self.S
        xb = self.XBUF[b]
        self.prefetched.add((special, seq, j, b))
        if special:
            src = self.xs[:, :]
        else:
            r0 = seq * SEQ + j * 512 + b * 128
            src = self.xp[r0:r0 + 128, :]
        S.op("sp", lambda h: h.dma_start(out=self.xin[:, xb, :], in_=src), writes=[("xin", xb)], dsem=self.x_sem[xb])
        for c4 in range(4):
            S.op("dve", lambda h, c4=c4: h.bn_stats(out=self.bst[:, xb, c4, :], in_=self.xin[:, xb, c4 * 512:(c4 + 1) * 512]),
                 reads=[("xin", xb)], writes=[("bst", xb)])
        S.op("dve", lambda h: h.bn_aggr(out=self.bmv[:, xb, :], in_=self.bst[:, xb, :, :]), reads=[("bst", xb)], writes=[("bmv", xb)])
        S.op("dve", lambda h: h.tensor_scalar(out=self.brs[:, xb, 0:1], in0=self.bmv[:, xb, 1:2], scalar1=EPS, scalar2=None, op0=ALU.add),
             reads=[("bmv", xb)], writes=[("brs", xb)])
        S.op("act", lambda h: h.sqrt(out=self.brs[:, xb, 0:1], in_=self.brs[:, xb, 0:1]), reads=[("brs", xb)], writes=[("brs", xb)])
        S.op("dve", lambda h: h.reciprocal(out=self.brs[:, xb, 0:1], in_=self.brs[:, xb, 0:1]), reads=[("brs", xb)], writes=[("brs", xb)])
        S.op("dve", lambda h: h.scalar_tensor_tensor(out=self.brs[:, xb, 1:2], in0=self.bmv[:, xb, 0:1], scalar=-1.0, in1=self.brs[:, xb, 0:1],
                                                     op0=ALU.mult, op1=ALU.mult),
             reads=[("bmv", xb), ("brs", xb)], writes=[("brs", xb)])
        S.op("act", lambda h: h.activation(out=self.xin[:, xb, :], in_=self.xin[:, xb, :], func=AF.Identity,
                                           bias=self.brs[:, xb, 1:2], scale=self.brs[:, xb, 0:1]),
             reads=[("brs", xb), ("xin", xb)], writes=[("xin", xb)])

    def views(self, special):
        U = self.U
        if special:
            n = 128
            aT = U[:, 0:NFC * n].rearrange("p (k t) -> p k t", t=n)
            mixT = U[:, 0:16 * n].rearrange("p (k t) -> p k t", t=n)
            o = 16 * n
            uT = U[:, o:o + 2 * 8 * (n + HL)].bitcast(F32).rearrange("p (k t) -> p k t", t=n + HL)
            o += 2 * 8 * (n + HL)
            dT = U[:, o:o + 8 * n].rearrange("p (k t) -> p k t", t=n)
            assert o + 8 * n <= NFC * n
        else:
            aT = U[:, :].rearrange("p (k t) -> p k t", t=512)
            mixT = U[:, 0:8192].rearrange("p (k t) -> p k t", t=512)
            uT = U[:, 8192:8192 + 2 * 8 * (512 + HL)].bitcast(F32).rearrange("p (k t) -> p k t", t=512 + HL)
            o = 8192 + 2 * 8 * (512 + HL)
            dT = U[:, o:o + 4096].rearrange("p (k t) -> p k t", t=512)
        return aT, mixT, uT, dT

    def rot(self, key, n=2):
        v = self.rr[key]; self.rr[key] = (v + 1) % n
        return v

    def prologue(self):
        S = self.S
        S.op("pool", lambda h: h.dma_start(out=self.wp[:], in_=self.wpool[:, :]), writes=["wp"], dsem=self.c_sem)
        for ci, (a, b) in enumerate(CONV_CHUNKS):
            for a2 in range(a, b, 4):
                b2 = min(b, a2 + 4)
                src = self.wt[a2:b2].rearrange("u p e -> (u p) e"); dst = self.wsc[a2:b2].rearrange("u p e -> (u p) e")
                S.op("pool", lambda h, s=src, d=dst: h.dma_start(out=d, in_=s), writes=[("wsc", ci)], dsem=self.conv_sem[ci])
        cs = self.c_sem
        ld = lambda o, i, w: S.op("sp", lambda h: h.dma_start(out=o, in_=i), writes=[w], dsem=cs)
        ld(self.identf[:], self.c_ident[:, :], "identf")
        ld(self.cv[:], self.colvec[:, :], "cv")
        ld(self.bvt[:], self.bv.partition_broadcast(128), "bvt")
        ld(self.sinkx[:], self.sinks.partition_broadcast(128), "sinkx")
        S.op("dve", lambda h: h.memset(self.onesb[:], 1.0), writes=["onesb"])
        S.op("sp", lambda h: h.dma_start(out=self.tt[:, 0, 0:128], in_=self.c_perm[:, :]), writes=[("tt", 0)], dsem=cs)
        S.op("dve", lambda h: h.tensor_copy(out=self.permb[:], in_=self.tt[:, 0, 0:128]), reads=[("tt", 0)], writes=["permb"])
        for m in range(3):
            S.op("sp", lambda h, m=m: h.dma_start(out=self.tt[:, 0, :], in_=self.c_masks[m]), writes=[("tt", 0)], dsem=cs)
            S.op("dve", lambda h, m=m: h.tensor_copy(out=self.masks[:, m, :], in_=self.tt[:, 0, :]), reads=[("tt", 0)], writes=["masks"])
        tot = (cs, S.dcnt[id(cs)])
        for rname in ("identf", "cv", "bvt", "sinkx", "wp"):
            S.lastw[rname] = tot
        S.op("act", lambda h: h.activation(out=self.sinkx[:], in_=self.sinkx[:], func=AF.Exp), reads=["sinkx"], writes=["sinkx"])
        S.op("dve", lambda h: h.tensor_copy(out=self.identb[:], in_=self.identf[:]), reads=["identf"], writes=["identb"])
        S.op("dve", lambda h: h.memset(self.Vx[:], 1.0), writes=["Vx"])
        S.op("dve", lambda h: h.memset(self.kT[:], 0.0), writes=["kT"])
        S.op("sp", lambda h: h.dma_start(out=self.nks[:, 0:127, :], in_=self.ck[:, 1:128, :]), dsem=self.m_sem)
        S.op("sp", lambda h: h.dma_start(out=self.nvs[:, 0:127, :], in_=self.cvv[:, 1:128, :]), dsem=self.m_sem)
        S.op("sp", lambda h: h.dma_start(out=self.nps[:, 0:14, :], in_=self.spool[:, 1:15, :]), dsem=self.m_sem)
        self.ring_next = 0
        self.ring_use = 0
        self.plan = list(META_UNITS) + list(range(NUNIT)) * (self.n_reg_tiles + 1)
        self.total_units = len(self.plan)
        for s in range(R_SLOTS):
            self.prefetch(s)

    def prefetch(self, slot):
        if self.ring_next >= self.total_units:
            return
        g = self.ring_next; self.ring_next += 1
        u = self.plan[g]
        kind, idx, nk = UNITS[u]
        n = nk * 128 if kind != "v" else 2048
        ci = [i for i, (a, b) in enumerate(CONV_CHUNKS) if a <= u < b][0]
        self.S.op("sp", lambda h, u=u, n=n, slot=slot: h.dma_start(out=self.ring[:, slot, 0:n], in_=self.wsc[u, :, 0:n]),
                  reads=[("wsc", ci)], writes=[("ring", slot)], dsem=self.ring_sem[slot])

    def take(self, kind):
        g = self.ring_use; self.ring_use += 1
        assert UNITS[self.plan[g]][0] == kind, (UNITS[self.plan[g]], kind)
        return g % R_SLOTS

    def mm_S(self, kind, bank, rhs_of_kc, N, nk=16, kc0=0, start=True, stop=True):
        slot = self.take(kind)
        ring = self.ring; bk = self.bank[bank]

        def fn(h):
            ins = None
            for kc in range(nk):
                ins = h.matmul(bk[:, 0:N], ring[:, slot, kc * 128:(kc + 1) * 128], rhs_of_kc(kc0 + kc),
                               start=(start and kc == 0), stop=(stop and kc == nk - 1))
            return ins
        return slot, fn

    def ln_fm(self, N, gname, bname, src_res, evac_fn_list, tag):
        S = self.S
        b1, b2 = 4, 5
        for dc in range(16):
            i = self.rot("ln")
            S.op("act", lambda h, dc=dc, i=i: h.copy(out=self.rbf[:, i, 0:N], in_=self.resT[:, dc, 0:N]),
                 reads=[("res", dc)], writes=[("rbf", i)])
            S.op("act", lambda h, dc=dc, i=i: h.activation(out=self.rsq[:, i, 0:N], in_=self.resT[:, dc, 0:N], func=AF.Square),
                 reads=[("res", dc)], writes=[("rsq", i)])
            S.op("pe", lambda h, dc=dc, i=i: h.matmul(self.bank[b1][:, 0:N], self.onesb[:], self.rbf[:, i, 0:N],
                                                     start=(dc == 0), stop=(dc == 15)),
                 reads=[("rbf", i), "onesb"], writes=[("bank", b1)])
            S.op("pe", lambda h, dc=dc, i=i: h.matmul(self.bank[b2][:, 0:N], self.onesb[:], self.rsq[:, i, 0:N],
                                                     start=(dc == 0), stop=(dc == 15)),
                 reads=[("rsq", i), "onesb"], writes=[("bank", b2)])
        mean, rstd, tmp = self.st_mean, self.st_rstd, self.st_tmp
        S.op("dve", lambda h: h.tensor_scalar(out=mean[:, 0:N], in0=self.bank[b1][:, 0:N], scalar1=1.0 / D, scalar2=None, op0=ALU.mult),
             reads=[("bank", b1)], writes=["st_mean"])
        S.op("dve", lambda h: h.tensor_tensor(out=tmp[:, 0:N], in0=mean[:, 0:N], in1=mean[:, 0:N], op=ALU.mult),
             reads=["st_mean"], writes=["st_tmp"])
        S.op("dve", lambda h: h.scalar_tensor_tensor(out=rstd[:, 0:N], in0=self.bank[b2][:, 0:N], scalar=1.0 / D, in1=tmp[:, 0:N],
                                                     op0=ALU.mult, op1=ALU.subtract),
             reads=[("bank", b2), "st_tmp"], writes=["st_rstd"])
        S.op("dve", lambda h: h.tensor_scalar(out=rstd[:, 0:N], in0=rstd[:, 0:N], scalar1=EPS, scalar2=None, op0=ALU.add),
             reads=["st_rstd"], writes=["st_rstd"])
        S.op("act", lambda h: h.sqrt(out=rstd[:, 0:N], in_=rstd[:, 0:N]), reads=["st_rstd"], writes=["st_rstd"])
        S.op("dve", lambda h: h.reciprocal(out=rstd[:, 0:N], in_=rstd[:, 0:N]), reads=["st_rstd"], writes=["st_rstd"])
        g0, b0 = CV[gname], CV[bname]
        pool_ok = self.cur_tile >= 2 or self.cur_tile == 0
        bufs = []
        for dc in range(16):
            i4 = self.rot("tt4", 4)
            tb, rn = ((self.tt, ("tt", i4)) if i4 < 2 else (self.t2, ("t2", i4 - 2)))
            bufs.append((tb[:, i4 % 2, 0:N], rn))

        def e_sub(dc):
            tv, rn = bufs[dc]
            S.op("dve", lambda h: h.tensor_tensor(out=tv, in0=self.resT[:, dc, 0:N], in1=mean[:, 0:N], op=ALU.subtract),
                 reads=[("res", dc), "st_mean"], writes=[rn])

        def e_rest(dc):
            tv, rn = bufs[dc]
            S.op("dve", lambda h: h.tensor_tensor(out=tv, in0=tv, in1=rstd[:, 0:N], op=ALU.mult),
                 reads=[rn, "st_rstd"], writes=[rn])
            if tag == "ln2":
                S.op("act", lambda h: h.activation(out=self.resT[:, dc, 0:N], in_=tv, func=AF.Identity,
                                                   bias=self.cv[:, b0 + dc:b0 + dc + 1], scale=self.cv[:, g0 + dc:g0 + dc + 1]),
                     reads=[rn, "cv"], writes=[("res", dc)])
            else:
                S.op("act", lambda h: h.activation(out=self.hT[:, dc, 0:N], in_=tv, func=AF.Identity,
                                                   bias=self.cv[:, b0 + dc:b0 + dc + 1], scale=self.cv[:, g0 + dc:g0 + dc + 1]),
                     reads=[rn, "cv"], writes=[("hT", dc)])
        e_sub(0)
        for dc in range(16):
            if dc + 1 < 16:
                e_sub(dc + 1)
            e_rest(dc)
        if tag != "ln2":
            de = "dve"
            for dc in range(16):
                pi = self.rot("pl")
                pv = self.pA[:, pi, 0:N]
                S.op(de, lambda h, dc=dc, pv=pv: h.tensor_tensor(out=pv, in0=self.resT[:, dc, 0:N], in1=mean[:, 0:N], op=ALU.subtract),
                     reads=[("res", dc), "st_mean"], writes=[("pA", pi)])
                S.op(de, lambda h, pv=pv: h.tensor_tensor(out=pv, in0=pv, in1=rstd[:, 0:N], op=ALU.mult),
                     reads=[("pA", pi), "st_rstd"], writes=[("pA", pi)])
                S.op("act", lambda h, dc=dc, pv=pv: h.activation(out=self.resT[:, dc, 0:N], in_=pv, func=AF.Identity,
                                                                bias=self.cv[:, b0 + dc:b0 + dc + 1], scale=self.cv[:, g0 + dc:g0 + dc + 1]),
                     reads=[("pA", pi), "cv"], writes=[("res", dc)])

    def run_tiles(self):
        self.cur_tile = -1
        self.prefetched = set()
        self.tile(special=True, seq=None, j=None, meta_only=True)
        for t in range(self.n_reg_tiles):
            self.cur_tile = t + 1
            self.tile(special=False, seq=t // 4, j=t % 4)
        self.cur_tile = 0
        self.tile(special=True, seq=None, j=None)

    def tile(self, special, seq, j, meta_only=False):
        S = self.S
        nblk = 1 if special else 4
        N = 128 * nblk
        first = (not special) and j == 0
        last = (not special) and j == 3
        aT, mixT, uT, dT = self.views(special)
        self.v_aT, self.v_mixT, self.v_uT, self.v_dT = aT, mixT, uT, dT
        ti = 4 if special else j
        S.op("sp", lambda h: h.dma_start(out=self.rope[:, 0, :], in_=self.c_rope[ti, 0]), writes=["rope"], dsem=self.rope_sem)
        S.op("sp", lambda h: h.dma_start(out=self.rope[:, 1, :], in_=self.c_rope[ti, 1]), writes=["rope"], dsem=self.rope_sem)
        if first:
            S.op("dve", lambda h: h.tensor_copy(out=self.kT[:, :, 0:128], in_=self.metaK[:]), reads=["metaK"], writes=["kT"])
            S.op("dve", lambda h: h.tensor_copy(out=self.Vx[:, 0], in_=self.metaV[:]), reads=["metaV"], writes=["Vx"])
        g0c, b0c = CV["g0"], CV["b0"]
        for b in range(nblk):
            xb = self.XBUF[b]
            if special or (special, seq, j, b) not in self.prefetched:
                self.x_prefetch(special, seq, j, b)
            for q4 in range(4):
                bk = self.rot("A", 4)

                def fn(h, q4=q4, xb=xb, bk=bk):
                    ins = None
                    for i in range(4):
                        dc = q4 * 4 + i
                        ins = h.transpose(out=self.bank[bk][:, i * 128:(i + 1) * 128], in_=self.xin[:, xb, dc * 128:(dc + 1) * 128],
                                          identity=self.identf[:])
                    return ins
                S.op("pe", fn, reads=[("xin", xb), "identf"], writes=[("bank", bk)])

                def fev(h, q4=q4, b=b, bk=bk):
                    ins = None
                    for i in range(4):
                        dc = q4 * 4 + i
                        ins = h.activation(out=self.resT[:, dc, b * 128:(b + 1) * 128], in_=self.bank[bk][:, i * 128:(i + 1) * 128], func=AF.Identity,
                                           bias=self.cv[:, b0c + dc:b0c + dc + 1], scale=self.cv[:, g0c + dc:g0c + dc + 1])
                    return ins
                if q4 % 2 == 0:
                    S.op("act", fev, reads=[("bank", bk), "cv"], writes=[("res", q4 * 4 + i) for i in range(4)])
                else:
                    ro = self.resT[:, q4 * 4:q4 * 4 + 4, b * 128:(b + 1) * 128]
                    gb = self.cv[:, g0c + q4 * 4:g0c + q4 * 4 + 4].unsqueeze(2).to_broadcast([128, 4, 128])
                    bb = self.cv[:, b0c + q4 * 4:b0c + q4 * 4 + 4].unsqueeze(2).to_broadcast([128, 4, 128])
                    S.op("dve", lambda h, ro=ro, gb=gb, bk=bk: h.tensor_tensor(out=ro, in0=self.bank[bk][:, :].rearrange("p (c t) -> p c t", t=128), in1=gb, op=ALU.mult),
                         reads=[("bank", bk), "cv"], writes=[("res", q4 * 4 + i) for i in range(4)])
                    S.op("dve", lambda h, ro=ro, bb=bb: h.tensor_tensor(out=ro, in0=ro, in1=bb, op=ALU.add),
                         reads=[("res", q4 * 4 + i) for i in range(4)] + ["cv"], writes=[("res", q4 * 4 + i) for i in range(4)])
                ce = "dve" if q4 % 2 == 0 else "act"
                S.op(ce, lambda h, q4=q4, b=b, ce=ce: (h.tensor_copy if ce == "dve" else h.copy)(
                    out=self.hT[:, q4 * 4:q4 * 4 + 4, b * 128:(b + 1) * 128], in_=self.resT[:, q4 * 4:q4 * 4 + 4, b * 128:(b + 1) * 128]),
                    reads=[("res", q4 * 4 + i) for i in range(4)], writes=[("hT", q4 * 4 + i) for i in range(4)])
        if self.stage <= 2 and self.cur_tile == self.n_reg_tiles:
            return self.dump(2, N)
        rhs_h = lambda kc: self.hT[:, kc, 0:N]
        def rope_post(kind, c, zi):
            pb = 6 + (zi % 2)
            S.op("pe", lambda h, zi=zi, pb=pb: h.matmul(self.bank[pb][:, 0:N], self.permb[:], self.zb[:, zi, 0:N], start=True, stop=True),
                 reads=[("zb", zi), "permb"], writes=[("bank", pb)])
            S.op("dve", lambda h, zi=zi, pb=pb: h.tensor_tensor(out=self.t2[:, zi, 0:N], in0=self.bank[pb][:, 0:N], in1=self.rope[:, 1, 0:N], op=ALU.mult),
                 reads=[("bank", pb), "rope"], writes=[("t2", zi)])
            S.op("dve", lambda h, zi=zi: h.tensor_tensor(out=self.zf[:, zi, 0:N], in0=self.zf[:, zi, 0:N], in1=self.rope[:, 0, 0:N], op=ALU.mult),
                 reads=[("zf", zi), "rope"], writes=[("zf", zi)])
            if kind == "q":
                S.op("dve", lambda h, zi=zi, c=c: h.tensor_tensor(out=self.qT[:, c, 0:N], in0=self.zf[:, zi, 0:N], in1=self.t2[:, zi, 0:N], op=ALU.add),
                     reads=[("zf", zi), ("t2", zi)], writes=[("qT", c)])
            else:
                S.op("dve", lambda h, zi=zi, c=c: h.tensor_tensor(out=self.kT[:, c, 128:128 + N], in0=self.zf[:, zi, 0:N], in1=self.t2[:, zi, 0:N], op=ALU.add),
                     reads=[("zf", zi), ("t2", zi)], writes=["kT"])
                if last or (special and not meta_only):
                    S.op("dve", lambda h, zi=zi, c=c: h.tensor_tensor(out=self.kf[:, c, :], in0=self.zf[:, zi, N - 128:N], in1=self.t2[:, zi, N - 128:N], op=ALU.add),
                         reads=[("zf", zi), ("t2", zi)], writes=["st_tmp"])

        pend = None
        for kind, cnt, bcol in (("q", 8, CV["bq"]), ("k", 4, CV["bk"])):
            if meta_only and kind == "q":
                continue
            for c in range(cnt):
                bk = self.rot("A", 4)
                slot, fn = self.mm_S(kind, bk, rhs_h, N)
                S.op("pe", fn, reads=[("ring", slot)] + [("hT", dc) for dc in range(16)], writes=[("bank", bk)])
                self.prefetch(slot)
                zi = self.rot("z")
                S.op("act", lambda h, zi=zi, bk=bk, c=c, bcol=bcol: h.activation(out=self.zf[:, zi, 0:N], in_=self.bank[bk][:, 0:N], func=AF.Identity,
                                                                              bias=self.cv[:, bcol + c:bcol + c + 1], scale=1.0),
                     reads=[("bank", bk), "cv"], writes=[("zf", zi)])
                S.op("act", lambda h, zi=zi: h.copy(out=self.zb[:, zi, 0:N], in_=self.zf[:, zi, 0:N]), reads=[("zf", zi)], writes=[("zb", zi)])
                if pend is not None:
                    rope_post(*pend)
                pend = (kind, c, zi)
        rope_post(*pend)
        if last or (special and not meta_only):
            self.emit_new_k(seq, special)
        if self.stage <= 3 and self.cur_tile == self.n_reg_tiles:
            return self.dump(3, N)
        vslots = [self.take("v"), self.take("v")]
        vb = [self.rot("A", 4) for _ in range(nblk)]
        for hh in range(2):
            def fn(h, hh=hh):
                ins = None
                for b in range(nblk):
                    for kc in range(8):
                        ins = h.matmul(self.bank[vb[b]][:, 0:256], self.hT[:, hh * 8 + kc, b * 128:(b + 1) * 128],
                                       self.ring[:, vslots[hh], kc * 256:(kc + 1) * 256],
                                       start=(hh == 0 and kc == 0), stop=(hh == 1 and kc == 7))
                return ins
            S.op("pe", fn, reads=[("ring", vslots[hh])] + [("hT", dc) for dc in range(16)], writes=[("bank", x) for x in vb])
            self.prefetch(vslots[hh])
        for b in range(nblk):
            S.op("dve", lambda h, b=b: h.tensor_tensor(out=self.Vx[:, 1 + b, :, 0:64], in0=self.bank[vb[b]][:, 0:256].rearrange("p (g d) -> p g d", g=4),
                                                      in1=self.bvt[:].rearrange("p (g d) -> p g d", g=4), op=ALU.add),
                 reads=[("bank", vb[b]), "bvt"], writes=["Vx"])
            if (last and b == nblk - 1) or (special and not meta_only):
                S.op("dve", lambda h, b=b: h.tensor_tensor(out=self.vst[:, 0, :], in0=self.bank[vb[b]][:, 0:256], in1=self.bvt[:], op=ALU.add),
                     reads=[("bank", vb[b]), "bvt"], writes=["vst0"])
                if special:
                    S.op("sp", lambda h: h.dma_start(out=self.nvs[:, 127, :], in_=self.vst[0:NSAMP, 0, :]), reads=["vst0"], dsem=self.v_sem[0])
                else:
                    S.op("sp", lambda h: h.dma_start(out=self.nvp[seq], in_=self.vst[:, 0, :]), reads=["vst0"], dsem=self.v_sem[0])
        if self.stage <= 4 and self.cur_tile == self.n_reg_tiles:
            return self.dump(4, N)
        if special and not meta_only:
            self.sample_pool_prep()
        for f in range(8):
            bk = self.rot("A", 4)
            slot, fn = self.mm_S("u", bk, rhs_h, N)
            S.op("pe", fn, reads=[("ring", slot)] + [("hT", dc) for dc in range(16)], writes=[("bank", bk)])
            self.prefetch(slot)
            S.op("act", lambda h, f=f, bk=bk: h.activation(out=uT[:, f, HL:HL + N], in_=self.bank[bk][:, 0:N], func=AF.Identity,
                                                          bias=self.cv[:, CV["bu"] + f:CV["bu"] + f + 1], scale=1.0),
                 reads=[("bank", bk), "cv"], writes=["uT"])
            if meta_only:
                continue
            if special:
                S.op("dve", lambda h, f=f: h.memset(uT[:, f, 0:HL], 0.0), writes=["uT"])
            else:
                hsrc = self.metaU if first else self.uhist
                S.op("dve", lambda h, f=f, hsrc=hsrc: h.tensor_copy(out=uT[:, f, 1:HL], in_=hsrc[:, f, :]),
                     reads=["metaU", "uhist"], writes=["uT"])
            gi = f // 2
            L = HL + N
            srcap = uT[:, f, :]
            rd = ["uT"]
            for lvl in range(gi + 1):
                sh = 1 << lvl
                pi = self.rot("pl")
                dst = self.pA[:, pi, :]
                S.op("dve", lambda h, s=srcap, d=dst, sh=sh, L=L: h.tensor_tensor(out=d[:, sh:L], in0=s[:, sh:L], in1=s[:, 0:L - sh], op=ALU.add),
                     reads=rd, writes=[("pA", pi)])
                srcap = dst; rd = [("pA", pi)]
            w = 2 << gi
            S.op("dve", lambda h, s=srcap, f=f, w=w: h.scalar_tensor_tensor(out=dT[:, f, 0:N], in0=s[:, HL:HL + N], scalar=1.0 / w,
                                                                          in1=uT[:, f, HL:HL + N], op0=ALU.mult, op1=ALU.subtract),
                 reads=rd + ["uT"], writes=[("dT", f)])
            if special:
                ss = self.st_tmp
                S.op("dve", lambda h, f=f, w=w, ss=ss: h.tensor_reduce(out=ss[:, 0:NSAMP], in_=self.histT[:, f, :, 16 - w:15], axis=AX.X, op=ALU.add),
                     reads=["histT"], writes=["st_tmp"])
                S.op("dve", lambda h, f=f, ss=ss: h.tensor_tensor(out=ss[:, 0:NSAMP], in0=ss[:, 0:NSAMP], in1=uT[:, f, HL:HL + NSAMP], op=ALU.add),
                     reads=["st_tmp", "uT"], writes=["st_tmp"])
                S.op("dve", lambda h, f=f, w=w, ss=ss: h.scalar_tensor_tensor(out=dT[:, f, 0:NSAMP], in0=ss[:, 0:NSAMP], scalar=1.0 / w,
                                                                             in1=uT[:, f, HL:HL + NSAMP], op0=ALU.mult, op1=ALU.subtract),
                     reads=["st_tmp", "uT"], writes=[("dT", f)])
        if meta_only:
            S.op("dve", lambda h: h.tensor_copy(out=self.metaU[:], in_=uT[:, :, HL + 113:HL + 128]), reads=["uT"], writes=["metaU"])
            S.op("dve", lambda h: h.tensor_copy(out=self.metaK[:], in_=self.kT[:, :, 128:256]), reads=["kT"], writes=["metaK"])
            S.op("dve", lambda h: h.tensor_copy(out=self.metaV[:], in_=self.Vx[:, 1]), reads=["Vx"], writes=["metaV"])
            return
        if special:
            self.emit_new_pool(None, N, uT, sample=True)
        elif last:
            self.emit_new_pool(seq, N, uT)
        else:
            S.op("dve", lambda h: h.tensor_copy(out=self.uhist[:], in_=uT[:, :, N + 1:N + HL]), reads=["uT"], writes=["uhist"])
        if self.stage <= 5 and self.cur_tile == self.n_reg_tiles:
            return self.dump(5, N)
        for oc in range(8):
            gi = oc // 2; o2 = oc % 2
            bk = self.rot("A", 4)

            def fn(h, gi=gi, o2=o2, bk=bk):
                ins = None
                for kc in range(2):
                    c0 = gi * 512 + kc * 256 + o2 * 128
                    ins = h.matmul(self.bank[bk][:, 0:N], self.wp[:, c0:c0 + 128], dT[:, gi * 2 + kc, 0:N], start=(kc == 0), stop=(kc == 1))
                return ins
            S.op("pe", fn, reads=["wp", ("dT", gi * 2), ("dT", gi * 2 + 1)], writes=[("bank", bk)])
            S.op("dve", lambda h, oc=oc, bk=bk: h.tensor_scalar(out=mixT[:, 8 + oc, 0:N], in0=self.bank[bk][:, 0:N],
                                                               scalar1=self.cv[:, CV["ps"] + oc:CV["ps"] + oc + 1], scalar2=None, op0=ALU.mult),
                 reads=[("bank", bk), "cv"], writes=[("mixT", 8 + oc)])
        if self.stage <= 6 and self.cur_tile == self.n_reg_tiles:
            return self.dump(6, N)
        if special:
            self.sample_cache_prep()
        mks = [2 if special else (1 if (first and b == 0) else 0) for b in range(nblk)]
        self.attention_tile(nblk, mks, mixT, special)
        if (not special) and (not last):
            S.op("dve", lambda h: h.tensor_copy(out=self.kT[:, :, 0:128], in_=self.kT[:, :, 512:640]), reads=["kT"], writes=["kT"])
            S.op("dve", lambda h: h.tensor_copy(out=self.Vx[:, 0], in_=self.Vx[:, 4]), reads=["Vx"], writes=["Vx"])
        if self.stage <= 7 and self.cur_tile == self.n_reg_tiles:
            return self.dump(7, N)
        rhs_m = lambda kc: mixT[:, kc, 0:N]
        for dc in range(16):
            bk = self.rot("A", 4)
            slot, fn = self.mm_S("o", bk, rhs_m, N)
            S.op("pe", fn, reads=[("ring", slot)] + [("mixT", k) for k in range(16)], writes=[("bank", bk)])
            self.prefetch(slot)
            S.op("dve", lambda h, dc=dc, bk=bk: h.scalar_tensor_tensor(out=self.resT[:, dc, 0:N], in0=self.resT[:, dc, 0:N], scalar=ALPHA,
                                                                      in1=self.bank[bk][:, 0:N], op0=ALU.mult, op1=ALU.add),
                 reads=[("bank", bk), ("res", dc)], writes=[("res", dc)])
        self.ln_fm(N, "g1", "b1", None, None, "ln1")
        if self.stage <= 8 and self.cur_tile == self.n_reg_tiles:
            return self.dump(8, N)
        for f in range(NFC):
            bg = self.rot("G", 2); bu = 2 + self.rot("Bup", 2)
            slot, fn = self.mm_S("g", bg, rhs_h, N)
            S.op("pe", fn, reads=[("ring", slot)] + [("hT", dc) for dc in range(16)], writes=[("bank", bg)])
            self.prefetch(slot)
            slot, fn = self.mm_S("up", bu, rhs_h, N)
            S.op("pe", fn, reads=[("ring", slot)] + [("hT", dc) for dc in range(16)], writes=[("bank", bu)])
            self.prefetch(slot)
            i = self.rot("tt")
            S.op("act", lambda h, i=i, bg=bg: h.activation(out=self.tt[:, i, 0:N], in_=self.bank[bg][:, 0:N], func=AF.Silu),
                 reads=[("bank", bg)], writes=[("tt", i)])
            S.op("dve", lambda h, i=i, bu=bu, f=f: h.tensor_tensor(out=aT[:, f, 0:N], in0=self.tt[:, i, 0:N], in1=self.bank[bu][:, 0:N], op=ALU.mult),
                 reads=[("tt", i), ("bank", bu)], writes=[("aT", f)])
        if self.stage <= 9 and self.cur_tile == self.n_reg_tiles:
            return self.dump(9, N)
        if (not special) and self.cur_tile < self.n_reg_tiles:
            nt_ = self.cur_tile
            for pb_ in range(2):
                self.x_prefetch(False, nt_ // 4, nt_ % 4, pb_)
        for dc in range(16):
            bk = self.rot("A", 4)
            for part in range(3):
                nk = 16 if part < 2 else 12
                slot, fn = self.mm_S("d", bk, lambda kc: aT[:, kc, 0:N], N, nk=nk, kc0=part * 16, start=(part == 0), stop=(part == 2))
                S.op("pe", fn, reads=[("ring", slot)] + [("aT", part * 16 + k) for k in range(nk)], writes=[("bank", bk)])
                self.prefetch(slot)
            S.op("dve", lambda h, dc=dc, bk=bk: h.scalar_tensor_tensor(out=self.resT[:, dc, 0:N], in0=self.resT[:, dc, 0:N], scalar=ALPHA,
                                                                      in1=self.bank[bk][:, 0:N], op0=ALU.mult, op1=ALU.add),
                 reads=[("bank", bk), ("res", dc)], writes=[("res", dc)])
        self.ln_fm(N, "g2", "b2", None, None, "ln2")
        if self.stage <= 10 and self.cur_tile == self.n_reg_tiles:
            return self.dump(10, N)
        for b in range(nblk):
            yb = 2
            for q4 in range(4):
                bk = self.rot("A", 4)

                def fn(h, q4=q4, b=b, bk=bk):
                    ins = None
                    for i in range(4):
                        dc = q4 * 4 + i
                        ins = h.transpose(out=self.bank[bk][:, i * 128:(i + 1) * 128], in_=self.resT[:, dc, b * 128:(b + 1) * 128],
                                          identity=self.identf[:])
                    return ins
                S.op("pe", fn, reads=[("res", q4 * 4 + i) for i in range(4)] + ["identf"], writes=[("bank", bk)])
                eng = "act" if q4 % 2 == 0 else "dve"
                if eng == "act":
                    S.op("act", lambda h, yb=yb, q4=q4, bk=bk: h.copy(out=self.xin[:, yb, q4 * 512:(q4 + 1) * 512], in_=self.bank[bk][:, :]),
                         reads=[("bank", bk)], writes=[("xin", yb)])
                else:
                    S.op("dve", lambda h, yb=yb, q4=q4, bk=bk: h.tensor_copy(out=self.xin[:, yb, q4 * 512:(q4 + 1) * 512], in_=self.bank[bk][:, :]),
                         reads=[("bank", bk)], writes=[("xin", yb)])
            if special:
                S.op("sp", lambda h, yb=yb: h.dma_start(out=self.ys[:, :], in_=self.xin[0:NSAMP, yb, :]), reads=[("xin", yb)], dsem=self.y_sem[yb])
            else:
                r0 = seq * SEQ + j * 512 + b * 128
                S.op("sp", lambda h, yb=yb, r0=r0: h.dma_start(out=self.yp[r0:r0 + 128, :], in_=self.xin[:, yb, :]), reads=[("xin", yb)], dsem=self.y_sem[yb])

    def dump(self, st, N):
        S = self.S
        if self.dbg is None or getattr(self, "dumped", False):
            return
        self.dumped = True
        if st in (1, 2, 8, 10):
            S.op("sp", lambda h: h.dma_start(out=self.dbg.rearrange("p (c t) -> p c t", t=512), in_=self.resT[:]),
                 reads=[("res", dc) for dc in range(16)], dsem=self.m_sem)
        elif st in (3, 4):
            S.op("dve", lambda h: h.tensor_copy(out=self.resT[:, 0:8, :], in_=self.qT[:]), reads=[("qT", c) for c in range(8)], writes=[("res", dc) for dc in range(16)])
            S.op("dve", lambda h: h.tensor_copy(out=self.resT[:, 8:12, :], in_=self.kT[:, :, 128:640]), reads=["kT"], writes=[("res", dc) for dc in range(16)])
            S.op("dve", lambda h: h.tensor_copy(out=self.resT[:, 12:14, 0:260], in_=self.Vx[:, 1:3].rearrange("p a g d -> p a (g d)")), reads=["Vx"], writes=[("res", dc) for dc in range(16)])
            S.op("sp", lambda h: h.dma_start(out=self.dbg.rearrange("p (c t) -> p c t", t=512), in_=self.resT[:]),
                 reads=[("res", dc) for dc in range(16)], dsem=self.m_sem)
        elif st in (5, 6, 7):
            S.op("dve", lambda h: h.tensor_copy(out=self.resT[:], in_=self.v_mixT[:]), reads=[("mixT", c) for c in range(16)] + [("dT", f) for f in range(8)], writes=[("res", dc) for dc in range(16)])
            S.op("sp", lambda h: h.dma_start(out=self.dbg.rearrange("p (c t) -> p c t", t=512), in_=self.resT[:]),
                 reads=[("res", dc) for dc in range(16)], dsem=self.m_sem)
        elif st == 9:
            S.op("dve", lambda h: h.tensor_copy(out=self.resT[:], in_=self.v_aT[:, 0:16, :]), reads=[("aT", f) for f in range(NFC)], writes=[("res", dc) for dc in range(16)])
            S.op("sp", lambda h: h.dma_start(out=self.dbg.rearrange("p (c t) -> p c t", t=512), in_=self.resT[:]),
                 reads=[("res", dc) for dc in range(16)], dsem=self.m_sem)

    def sample_pool_prep(self):
        S = self.S
        for half in range(2):
            xb = 1 + self.rot("x")
            S.op("sp", lambda h, xb=xb, half=half: h.dma_start(out=self.xin[0:120, xb, 0:1024],
                                                                in_=self.spool[half * 8:(half + 1) * 8].rearrange("b r c -> (b r) c")),
                 writes=[("xin", xb)], dsem=self.x_sem[xb])
            for q2 in range(2):
                bk = self.rot("A", 4)

                def fn(h, q2=q2, xb=xb, bk=bk):
                    ins = None
                    for i in range(4):
                        f = q2 * 4 + i
                        ins = h.transpose(out=self.bank[bk][:, i * 120:(i + 1) * 120], in_=self.xin[0:120, xb, f * 128:(f + 1) * 128],
                                          identity=self.identf[0:120, 0:120])
                    return ins
                S.op("pe", fn, reads=[("xin", xb), "identf"], writes=[("bank", bk)])
                S.op("dve", lambda h, q2=q2, half=half, bk=bk: h.tensor_copy(
                    out=self.histT[:, q2 * 4:q2 * 4 + 4, half * 8:(half + 1) * 8, :],
                    in_=self.bank[bk][:, 0:480].rearrange("p (f b r) -> p f b r", f=4, b=8)),
                    reads=[("bank", bk)], writes=["histT"])

    def sample_cache_prep(self):
        S = self.S
        S.op("dve", lambda h: h.memset(self.cVe[:], 1.0), reads=["histT"], writes=["cVe"])
        S.op("dve", lambda h: h.memset(self.Pexp[:], 0.0), reads=["histT"], writes=["Pexp", "histT"])
        for r in range(2):
            xb = 1 + self.rot("x")
            S.op("sp", lambda h, xb=xb, r=r: h.dma_start(out=self.xin[:, xb, :].rearrange("p (b c) -> p b c", b=8),
                                                          in_=self.cvv[r * 8:(r + 1) * 8].rearrange("b k c -> k b c")),
                 writes=[("xin", xb)], dsem=self.x_sem[xb])
            S.op("dve", lambda h, xb=xb, r=r: h.tensor_copy(out=self.cVe[:, r * 8:(r + 1) * 8, :, 0:64],
                                                            in_=self.xin[:, xb, :].rearrange("p (b g d) -> p b g d", b=8, g=4)),
                 reads=[("xin", xb)], writes=["cVe"])
        aof = self.ao[:].rearrange("p a n -> p (a n)")
        ckd = aof.rearrange("p (b g t d) -> p b g t d", b=4, g=4, t=2)
        for r in range(4):
            xb = 1 + self.rot("x")
            S.op("sp", lambda h, xb=xb, r=r: h.dma_start(out=self.xin[:, xb, 0:1024].rearrange("p (b c) -> p b c", b=4),
                                                          in_=self.ck[r * 4:(r + 1) * 4].rearrange("b k c -> k b c")),
                 writes=[("xin", xb)], dsem=self.x_sem[xb])
            for t in range(2):
                S.op("dve", lambda h, xb=xb, t=t: h.tensor_copy(out=ckd[:, :, :, t, :],
                                                                in_=self.xin[:, xb, 0:1024].rearrange("p (b g d) -> p b g d", b=4, g=4)),
                     reads=[("xin", xb)], writes=[("ao", 0), ("ao", 1)])
            for bl in range(4):
                bk = self.rot("A", 4)

                def fn(h, bl=bl, bk=bk):
                    ins = None
                    for g in range(4):
                        ins = h.matmul(self.bank[bk][:, g * 128:(g + 1) * 128], ckd[:, bl, g].rearrange("p t d -> p (t d)"), self.identb[:],
                                       start=True, stop=True)
                    return ins
                S.op("pe", fn, reads=[("ao", 0), ("ao", 1), "identb"], writes=[("bank", bk)])
                S.op("act", lambda h, r=r, bl=bl, bk=bk: h.copy(out=self.cKT[:, r * 4 + bl, :, :],
                                                               in_=self.bank[bk][:, :].rearrange("p (g k) -> p g k", g=4)),
                     reads=[("bank", bk)], writes=["cKT"])

    def attention_tile(self, nblk, mks, mixT, special=False):
        S = self.S
        prev = None
        ais = [None] * nblk
        for b in range(nblk):
            ais[b] = self.rot("ao")
            for g in range(4):
                st = self.att_stage1(b, g, mks[b], special)
                if prev is not None:
                    self.att_stage2(prev, ais, mixT, special)
                prev = st
        self.att_stage2(prev, ais, mixT, special)

    def att_stage1(self, b, g, mk, special):
        S = self.S
        qc = slice(b * 128, (b + 1) * 128)
        prevc = slice(b * 128, (b + 1) * 128)
        curc = slice((b + 1) * 128, (b + 2) * 128)
        r2 = self.rot("att")
        sbk = (r2, 6 + r2)

        def fqk(h):
            ins = None
            for pr in range(2):
                c = 2 * g + pr
                for hh in range(2):
                    ps = slice(hh * 64, hh * 64 + 64)
                    st = self.bank[sbk[hh]]
                    if special:
                        for sb in range(NSAMP):
                            h.matmul(st[:, pr * 256 + sb:pr * 256 + sb + 1], self.cKT[ps, sb, g, :], self.qT[ps, c, sb:sb + 1], start=True, stop=True)
                    else:
                        h.matmul(st[:, pr * 256:pr * 256 + 128], self.kT[ps, g, prevc], self.qT[ps, c, qc], start=True, stop=True)
                    ins = h.matmul(st[:, pr * 256 + 128:pr * 256 + 256], self.kT[ps, g, curc], self.qT[ps, c, qc], start=True, stop=True)
            return ins
        S.op("pe", fqk, reads=["kT", "cKT", ("qT", 2 * g), ("qT", 2 * g + 1)], writes=[("bank", sbk[0]), ("bank", sbk[1])])
        pis = []
        for hh in range(2):
            st = self.bank[sbk[hh]]
            pi = self.rot("pt", 4)
            pis.append(pi)
            S.op("act", lambda h, pi=pi, st=st: h.activation(out=self.pT[:, pi, :], in_=st[:, :], func=AF.Exp, scale=0.125),
                 reads=[("bank", sbk[hh])], writes=[("pT", pi)])
            S.op("dve", lambda h, pi=pi: h.tensor_tensor(out=self.pT[:, pi, :], in0=self.pT[:, pi, :], in1=self.masks[:, mk, :], op=ALU.mult),
                 reads=[("pT", pi), "masks"], writes=[("pT", pi)])
        return (b, g, pis)

    def att_stage2(self, st, ais, mixT, special):
        S = self.S
        b, g, pis = st
        ai = ais[b]
        qc = slice(b * 128, (b + 1) * 128)
        pob = 4 + self.rot("po")
        po2b = 2 + self.rot("po2")
        for hh in range(2):
            pi = pis[hh]

            def fpv(h, pi=pi, hh=hh):
                ins = None
                for pr in range(2):
                    o = self.bank[pob][:, (pr * 2 + hh) * 65:(pr * 2 + hh) * 65 + 65]
                    if not special:
                        h.matmul(o, self.pT[:, pi, pr * 256:pr * 256 + 128], self.Vx[:, b, g, :], start=True, stop=False)
                    ins = h.matmul(o, self.pT[:, pi, pr * 256 + 128:pr * 256 + 256], self.Vx[:, b + 1, g, :], start=special, stop=True)
                return ins
            S.op("pe", fpv, reads=[("pT", pi), "Vx"], writes=[("bank", pob)])
            if special:
                for pr in range(2):
                    hd = 4 * g + pr * 2 + hh
                    S.op("dve", lambda h, pi=pi, pr=pr, hd=hd: h.tensor_copy(
                        out=self.Pexp[:, hd].rearrange("p b c -> p (b c)")[:, 0:256:17], in_=self.pT[:, pi, pr * 256:pr * 256 + NSAMP]),
                        reads=[("pT", pi)], writes=["Pexp"])

                def fpv2(h, hh=hh):
                    ins = None
                    for pr in range(2):
                        hd = 4 * g + pr * 2 + hh
                        o = self.bank[po2b][0:NSAMP, (pr * 2 + hh) * 65:(pr * 2 + hh) * 65 + 65]
                        for sb in range(NSAMP):
                            ins = h.matmul(o, self.Pexp[:, hd, sb, :], self.cVe[:, sb, g, :], start=(sb == 0), stop=(sb == NSAMP - 1))
                    return ins
                S.op("pe", fpv2, reads=["Pexp", "cVe"], writes=[("bank", po2b)])
        po4 = self.bank[pob][:, 0:260].rearrange("p (h d) -> p h d", h=4)
        di = self.rot("den")
        S.op("dve", lambda h: h.tensor_tensor(out=self.den[:, di, :].unsqueeze(2), in0=po4[:, :, 64:65],
                                              in1=self.sinkx[:, 4 * g:4 * g + 4].unsqueeze(2), op=ALU.add),
             reads=[("bank", pob), "sinkx"], writes=[("den", di)])
        S.op("dve", lambda h: h.reciprocal(out=self.den[:, di, :], in_=self.den[:, di, :]), reads=[("den", di)], writes=[("den", di)])
        S.op("dve", lambda h: h.tensor_tensor(
            out=self.ao[:, ai, g * 256:(g + 1) * 256].rearrange("p (h d) -> p h d", h=4), in0=po4[:, :, 0:64],
            in1=self.den[:, di, :].unsqueeze(2).to_broadcast([128, 4, 64]), op=ALU.mult),
            reads=[("bank", pob), ("den", di)], writes=[("ao", ai)])
        if special:
            tm = self.st_tmp
            tm4 = tm[0:NSAMP, 0:260].rearrange("p (h d) -> p h d", h=4)
            S.op("act", lambda h: h.copy(out=tm[0:NSAMP, 0:260], in_=self.bank[po2b][0:NSAMP, 0:260]),
                 reads=[("bank", po2b)], writes=["st_tmp"])
            S.op("dve", lambda h: h.tensor_tensor(out=tm[0:NSAMP, 0:260], in0=tm[0:NSAMP, 0:260], in1=self.bank[pob][0:NSAMP, 0:260], op=ALU.add),
                 reads=["st_tmp", ("bank", pob)], writes=["st_tmp"])
            S.op("dve", lambda h: h.tensor_tensor(out=self.den[0:NSAMP, di, :].unsqueeze(2), in0=tm4[:, :, 64:65],
                                                  in1=self.sinkx[0:NSAMP, 4 * g:4 * g + 4].unsqueeze(2), op=ALU.add),
                 reads=["st_tmp", "sinkx"], writes=[("den", di)])
            S.op("dve", lambda h: h.reciprocal(out=self.den[0:NSAMP, di, :], in_=self.den[0:NSAMP, di, :]), reads=[("den", di)], writes=[("den", di)])
            S.op("dve", lambda h: h.tensor_tensor(
                out=self.ao[0:NSAMP, ai, g * 256:(g + 1) * 256].rearrange("p (h d) -> p h d", h=4), in0=tm4[:, :, 0:64],
                in1=self.den[0:NSAMP, di, :].unsqueeze(2).to_broadcast([NSAMP, 4, 64]), op=ALU.mult),
                reads=["st_tmp", ("den", di)], writes=[("ao", ai)])
        if g == 3:
            bk = self.rot("A", 4)
            tpv = self.bank[bk][:, :].bitcast(BF16)

            def ftr(h):
                ins = None
                for c in range(8):
                    ins = h.transpose(out=tpv[:, c * 128:(c + 1) * 128], in_=self.ao[:, ai, c * 128:(c + 1) * 128], identity=self.identb[:])
                return ins
            S.op("pe", ftr, reads=[("ao", ai), "identb"], writes=[("bank", bk)])
            S.op("act", lambda h: h.copy(out=mixT[:, 0:8, qc], in_=tpv.rearrange("p (c t) -> p c t", t=128)),
                 reads=[("bank", bk)], writes=[("mixT", k) for k in range(8)])

    def emit_new_pool(self, seq, N, uT, sample=False):
        S = self.S
        yb = 2
        c0, c1, nr = (HL, HL + NSAMP, NSAMP) if sample else (N + 1, N + HL, 15)
        for half in range(2):
            bk = self.rot("A", 4)

            def fn(h, half=half, bk=bk):
                ins = None
                for i in range(4):
                    f = half * 4 + i
                    ins = h.transpose(out=self.bank[bk][0:nr, i * 128:(i + 1) * 128], in_=uT[:, f, c0:c1], identity=self.identf[:])
                return ins
            S.op("pe", fn, reads=["uT", "identf"], writes=[("bank", bk)])
            S.op("dve", lambda h, half=half, bk=bk, yb=yb: h.tensor_copy(out=self.xin[0:nr, yb, half * 512:(half + 1) * 512], in_=self.bank[bk][0:nr, :]),
                 reads=[("bank", bk)], writes=[("xin", yb)])
        if sample:
            S.op("sp", lambda h, yb=yb: h.dma_start(out=self.nps[:, 14, :], in_=self.xin[0:NSAMP, yb, 0:1024]), reads=[("xin", yb)], dsem=self.y_sem[yb])
        else:
            S.op("sp", lambda h, yb=yb: h.dma_start(out=self.npp[seq], in_=self.xin[0:15, yb, 0:1024]), reads=[("xin", yb)], dsem=self.y_sem[yb])

    def emit_new_k(self, seq, special=False):
        S = self.S
        bk = self.rot("A", 4)

        def fn(h, bk=bk):
            ins = None
            for g in range(4):
                ins = h.transpose(out=self.bank[bk][:, g * 64:(g + 1) * 64], in_=self.kf[0:64, g, :], identity=self.identf[0:64, 0:64])
            return ins
        S.op("pe", fn, reads=["st_tmp", "identf"], writes=[("bank", bk)])
        S.op("dve", lambda h, bk=bk: h.tensor_copy(out=self.vst[:, 1, :], in_=self.bank[bk][:, 0:256]), reads=[("bank", bk)], writes=["vst1"])
        if special:
            S.op("sp", lambda h: h.dma_start(out=self.nks[:, 127, :], in_=self.vst[0:NSAMP, 1, :]), reads=["vst1"], dsem=self.v_sem[1])
        else:
            S.op("sp", lambda h: h.dma_start(out=self.nkp[seq], in_=self.vst[:, 1, :]), reads=["vst1"], dsem=self.v_sem[1])

    def epilogue(self):
        S = self.S
        for s in self.y_sem + [self.m_sem] + self.v_sem:
            if S.dcnt[id(s)] > 0:
                S.wait_tok("sp", (s, S.dcnt[id(s)]))


_CACHE = {}


def _program(n_reg_tiles=8, stage=99):
    if (n_reg_tiles, stage) not in _CACHE:
        _CACHE[(n_reg_tiles, stage)] = Builder(n_reg_tiles, stage).build()
    return _CACHE[(n_reg_tiles, stage)]


def kernel(x_prompt, x_sample, cache_k, cache_v, state_pool, meta_tokens, ln_in_g, ln_in_b,
           w_in, b_in, attn_sinks, w_pool, pool_scale, w_o, ln1_g, ln1_b,
           w_gate, w_up, w_down, ln2_g, ln2_b, _n_reg_tiles=8, _stage=99, _dbg=None, _ncores=NCORE):
    f = lambda a: np.ascontiguousarray(np.asarray(a, dtype=np.float32))
    x_prompt = f(x_prompt); x_sample = f(x_sample); cache_k = f(cache_k); cache_v = f(cache_v); state_pool = f(state_pool)
    wt = host_weights(f(w_in)[0], f(w_o)[0], f(w_gate)[0], f(w_up)[0], f(w_down)[0])
    wpool = np.ascontiguousarray(f(w_pool)[0].reshape(4, 2, 128, 256).transpose(2, 0, 1, 3).reshape(128, 2048))
    colvec = host_colvec(f(b_in)[0], f(pool_scale)[0], f(ln_in_g), f(ln_in_b), f(ln1_g)[0], f(ln1_b)[0], f(ln2_g)[0], f(ln2_b)[0])
    bv = np.ascontiguousarray(f(b_in)[0][1280:1536]); sinks = np.ascontiguousarray(f(attn_sinks)[0])
    ident, perm, masks, rope = host_consts()
    meta = f(meta_tokens)
    in_maps = []
    for c in range(NCORE):
        xs = np.zeros((128, D), np.float32)
        xs[0:NSAMP] = x_sample[c * NSAMP:(c + 1) * NSAMP, 0]
        xs[112:128] = meta
        in_maps.append({
            "xp": x_prompt[2 * c:2 * c + 2].reshape(2 * SEQ, D), "xs": xs,
            "ck": cache_k[0, c * NSAMP:(c + 1) * NSAMP].reshape(NSAMP, 128, 256),
            "cvv": cache_v[0, c * NSAMP:(c + 1) * NSAMP].reshape(NSAMP, 128, 256),
            "spool": state_pool[0, c * NSAMP:(c + 1) * NSAMP],
            "wt": wt, "wpool": wpool, "colvec": colvec, "bv": bv, "sinks": sinks,
            "c_ident": ident, "c_perm": perm, "c_masks": masks, "c_rope": rope,
        })
    nc = _program(_n_reg_tiles, _stage)
    in_maps = in_maps[:_ncores]
    res = run_bass_kernel_spmd(nc, in_maps, core_ids=list(range(_ncores)))
    R = res.results
    if _dbg is not None:
        _dbg.extend([r.get('dbg') for r in R])
    cat = lambda k: np.concatenate([r[k] for r in R] + [np.zeros_like(R[0][k])] * (NCORE - _ncores), 0)
    y_prompt = cat("yp").reshape(16, SEQ, D)
    y_sample = cat("ys").reshape(128, 1, D)
    nkp = cat("nkp").reshape(1, 16, 128, 4, 64); nvp = cat("nvp").reshape(1, 16, 128, 4, 64)
    npp = cat("npp").reshape(1, 16, 15, 1024)
    nks = cat("nks").reshape(1, 128, 128, 4, 64); nvs = cat("nvs").reshape(1, 128, 128, 4, 64)
    nps = cat("nps").reshape(1, 128, 15, 1024)
    return (y_prompt, y_sample, nkp, nvp, npp, nks, nvs, nps)
```

```python
import contextlib, os
import numpy as np
import concourse.bass as bass
import concourse.mybir as mybir
from concourse.bass_utils import run_bass_kernel_spmd

F32 = mybir.dt.float32
BF16 = mybir.dt.bfloat16
AF = mybir.ActivationFunctionType
ALU = mybir.AluOpType
AX = mybir.AxisListType

D = 2048; DFF = 5632; NKC = 16; NFC = 44; P = 128
NCORE = 8; SEQ = 2048; NSAMP = 16; NMETA = 16
ALPHA = float(2.0 ** 0.25)
EPS = 1e-5
HL = 16
R_SLOTS = 5
UNIT = 2048

def unit_table():
    u = []
    for c in range(8): u.append(("q", c, 16))
    for g in range(4): u.append(("k", g, 16))
    for h in range(2): u.append(("v", h, 8))
    for f in range(8): u.append(("u", f, 16))
    for dc in range(16): u.append(("o", dc, 16))
    for f in range(NFC):
        u.append(("g", f, 16)); u.append(("up", f, 16))
    for dc in range(16):
        u.append(("d", (dc, 0), 16)); u.append(("d", (dc, 1), 16)); u.append(("d", (dc, 2), 12))
    return u

UNITS = unit_table()
NUNIT = len(UNITS)
CONV_CHUNKS = [(0, 22), (22, 38), (38, 66), (66, 96), (96, 126), (126, 150), (150, NUNIT)]
META_UNITS = [i for i, u in enumerate(UNITS) if u[0] in ("k", "v", "u")]


def _tileS(W, c0, k0, k1, ncol=128):
    nk = k1 - k0
    a = W[k0 * 128:k1 * 128, c0:c0 + ncol].reshape(nk, 128, ncol).transpose(1, 0, 2).reshape(128, nk * ncol)
    out = np.zeros((128, UNIT), np.float32)
    out[:, :nk * ncol] = a
    return out


def host_weights(w_in, w_o, w_gate, w_up, w_down):
    wt = np.zeros((NUNIT, 128, UNIT), np.float32)
    for i, (kind, idx, nk) in enumerate(UNITS):
        if kind == "q":
            wt[i] = _tileS(w_in, idx * 128, 0, 16)
        elif kind == "k":
            cols = w_in[:, 1024 + idx * 64:1024 + idx * 64 + 64]
            wt[i] = _tileS(np.concatenate([cols, cols], 1), 0, 0, 16)
        elif kind == "v":
            wt[i] = _tileS(w_in, 1280, idx * 8, idx * 8 + 8, ncol=256)
        elif kind == "u":
            wt[i] = _tileS(w_in, 1536 + idx * 128, 0, 16)
        elif kind == "o":
            wt[i] = _tileS(w_o, idx * 128, 0, 16)
        elif kind == "g":
            wt[i] = _tileS(w_gate, idx * 128, 0, 16)
        elif kind == "up":
            wt[i] = _tileS(w_up, idx * 128, 0, 16)
        elif kind == "d":
            dc, part = idx
            k0 = part * 16
            wt[i] = _tileS(w_down, dc * 128, k0, k0 + nk)
    return wt


CV = {}
def _cv_layout():
    off = 0
    for name, n in (("bq", 8), ("bk", 4), ("bu", 8), ("ps", 8), ("g0", 16), ("b0", 16), ("g1", 16), ("b1", 16),
                    ("g2", 16), ("b2", 16)):
        CV[name] = off; off += n
    return off
NCV = _cv_layout()


def host_colvec(b_in, pool_scale, ln_in_g, ln_in_b, ln1_g, ln1_b, ln2_g, ln2_b):
    cv = np.zeros((128, NCV), np.float32)
    cv[:, CV["bq"]:CV["bq"] + 8] = b_in[0:1024].reshape(8, 128).T
    bk = b_in[1024:1280].reshape(4, 64)
    cv[:, CV["bk"]:CV["bk"] + 4] = np.concatenate([bk, bk], 1).T
    cv[:, CV["bu"]:CV["bu"] + 8] = b_in[1536:2560].reshape(8, 128).T
    cv[:, CV["ps"]:CV["ps"] + 8] = pool_scale.reshape(8, 128).T
    for nm, v in (("g0", ln_in_g), ("b0", ln_in_b), ("g1", ln1_g), ("b1", ln1_b), ("g2", ln2_g), ("b2", ln2_b)):
        cv[:, CV[nm]:CV[nm] + 16] = v.reshape(16, 128).T
    return cv


def host_consts():
    ident = np.eye(128, dtype=np.float32)
    perm = np.zeros((128, 128), np.float32)
    for m in range(128):
        d = m % 64
        if d < 8: perm[m + 8, m] = 1.0
        elif d < 16: perm[m - 8, m] = 1.0
    pi = np.arange(128)[:, None]; qi = np.arange(128)[None, :]
    prev = (pi >= qi).astype(np.float32); cur = (pi <= qi).astype(np.float32)
    reg = np.concatenate([prev, cur], 1)
    first = np.concatenate([prev * (pi >= 112), cur], 1)
    sprev = np.zeros((128, 128), np.float32); sprev[:, 0:16] = 1.0
    samp = np.concatenate([sprev, np.eye(128, dtype=np.float32)], 1)
    masks = np.stack([np.concatenate([m, m], 1) for m in (reg, first, samp)])
    inv = (np.float32(500000.0) ** (-np.arange(8, dtype=np.float32) * np.float32(2.0) / np.float32(16.0))).astype(np.float32)
    rope = np.zeros((5, 2, 128, 512), np.float32)
    rope[:, 0] = 1.0
    for t in range(5):
        if t < 4:
            pos = (16 + 512 * t + np.arange(512)).astype(np.float32)
        else:
            pos = np.zeros(512, np.float32)
            pos[0:16] = 16384.0
            pos[112:128] = np.arange(16, dtype=np.float32)
        ang = (pos[None, :] * inv[:, None]).astype(np.float32)
        c = np.cos(ang.astype(np.float64)).astype(np.float32)
        s = np.sin(ang.astype(np.float64)).astype(np.float32)
        for base in (0, 64):
            rope[t, 0, base:base + 8] = c; rope[t, 0, base + 8:base + 16] = c
            rope[t, 1, base:base + 8] = -s; rope[t, 1, base + 8:base + 16] = s
    return ident, perm, masks, rope


class Sched:
    ENGS = ("pe", "act", "dve", "pool", "sp")

    def __init__(self, nc, es):
        self.nc = nc; self.es = es
        self.q = {e: [] for e in self.ENGS}
        self.sem = {e: es.enter_context(nc.semaphore("prog_" + e)) for e in self.ENGS}
        self.cnt = {e: 0 for e in self.ENGS}
        self.known = {e: {} for e in self.ENGS}
        self.lastw = {}; self.readers = {}
        self.dcnt = {}

    def dma_sem(self, name):
        s = self.es.enter_context(self.nc.semaphore(name)); self.dcnt[id(s)] = 0
        return s

    def op(self, eng, fn, reads=(), writes=(), dsem=None):
        deps = []
        for r in reads:
            t = self.lastw.get(r)
            if t is not None: deps.append(t)
            if eng != "pe" and isinstance(r, tuple) and r[0] == "bank":
                deps.extend(self.readers.get(r, ()))
        for w in writes:
            t = self.lastw.get(w)
            if t is not None: deps.append(t)
            deps.extend(self.readers.get(w, ()))
        need = {}
        for (s, v) in deps:
            if need.get(id(s), (None, 0))[1] < v: need[id(s)] = (s, v)
        kn = self.known[eng]
        for k, (s, v) in need.items():
            if eng == "pe" and s is self.sem["pe"]:
                continue
            if kn.get(k, 0) < v:
                kn[k] = v
                self.q[eng].append(lambda h, s=s, v=v: h.wait_ge(s, v))
        if dsem is not None:
            self.dcnt[id(dsem)] += 16; tok = (dsem, self.dcnt[id(dsem)])
            self.q[eng].append(lambda h, fn=fn, s=dsem: fn(h).then_inc(s, 16))
        else:
            self.cnt[eng] += 1; tok = (self.sem[eng], self.cnt[eng])
            self.q[eng].append(lambda h, fn=fn, s=self.sem[eng]: fn(h).then_inc(s, 1))
        for r in reads: self.readers.setdefault(r, []).append(tok)
        for w in writes: self.lastw[w] = tok; self.readers[w] = []
        return tok

    def wait_tok(self, eng, tok):
        s, v = tok
        if self.known[eng].get(id(s), 0) < v:
            self.known[eng][id(s)] = v
            self.q[eng].append(lambda h, s=s, v=v: h.wait_ge(s, v))

    def emit(self):
        with self.nc.Block() as block:
            @block.tensor
            def _(h):
                for f in self.q["pe"]: f(h)

            @block.scalar
            def _(h):
                for f in self.q["act"]: f(h)

            @block.vector
            def _(h):
                for f in self.q["dve"]: f(h)

            @block.gpsimd
            def _(h):
                for f in self.q["pool"]: f(h)

            @block.sync
            def _(h):
                for f in self.q["sp"]: f(h)


class Builder:
    def __init__(self, n_reg_tiles=8, stage=99):
        self.n_reg_tiles = n_reg_tiles; self.stage = stage

    def build(self):
        nc = bass.Bass("TRN2", target_bir_lowering=False)
        self.nc = nc
        es = contextlib.ExitStack()
        with es:
            self.es = es
            self.S = Sched(nc, es)
            self.declare()
            self.prologue()
            self.run_tiles()
            self.epilogue()
            self.S.emit()
        return nc

    def declare(self):
        nc, es, S = self.nc, self.es, self.S
        di = lambda n, sh, dt=F32: nc.dram_tensor(n, sh, dt, kind="ExternalInput").ap()
        do = lambda n, sh: nc.dram_tensor(n, sh, F32, kind="ExternalOutput").ap()
        self.xp = di("xp", [2 * SEQ, D]); self.xs = di("xs", [128, D])
        self.ck = di("ck", [NSAMP, 128, 256]); self.cvv = di("cvv", [NSAMP, 128, 256])
        self.spool = di("spool", [NSAMP, 15, 1024])
        self.wt = di("wt", [NUNIT, 128, UNIT])
        self.wpool = di("wpool", [128, 2048])
        self.colvec = di("colvec", [128, NCV]); self.bv = di("bv", [256]); self.sinks = di("sinks", [16])
        self.c_ident = di("c_ident", [128, 128]); self.c_perm = di("c_perm", [128, 128])
        self.c_masks = di("c_masks", [3, 128, 512]); self.c_rope = di("c_rope", [5, 2, 128, 512])
        self.yp = do("yp", [2 * SEQ, D]); self.ys = do("ys", [NSAMP, D])
        self.nkp = do("nkp", [2, 128, 256]); self.nvp = do("nvp", [2, 128, 256]); self.npp = do("npp", [2, 15, 1024])
        self.nks = do("nks", [NSAMP, 128, 256]); self.nvs = do("nvs", [NSAMP, 128, 256])
        self.nps = do("nps", [NSAMP, 15, 1024])
        self.wsc = nc.dram_tensor("wsc", [NUNIT, 128, UNIT], BF16, kind="Internal").ap()
        self.dbg = do("dbg", [128, 16 * 512]) if self.stage < 99 else None

        sb = lambda n, sh, dt: es.enter_context(nc.sbuf_tensor(n, sh, dt))
        self.resT = sb("resT", [128, 16, 512], F32)
        self.hT = sb("hT", [128, 16, 512], BF16)
        self.U = sb("U", [128, NFC * 512], BF16)
        o = NFC * 128
        self.cKT = self.U[:, o:o + 8192].rearrange("p (b g k) -> p b g k", b=16, g=4); o += 8192
        self.cVe = self.U[:, o:o + 4160].rearrange("p (b g d) -> p b g d", b=16, g=4); o += 4160
        self.Pexp = self.U[:, o:o + 4096].rearrange("p (h b c) -> p h b c", h=16, b=16)
        self.histT = self.U[:, o:o + 3840].bitcast(F32).rearrange("p (f b r) -> p f b r", f=8, b=16)
        assert o + 4096 <= NFC * 512
        self.qT = sb("qT", [128, 8, 512], BF16)
        self.kT = sb("kT", [128, 4, 5 * 128], BF16)
        self.Vx = sb("Vx", [128, 5, 4, 65], BF16)
        self.zf = sb("zf", [128, 2, 512], F32); self.zb = sb("zb", [128, 2, 512], BF16)
        self.t2 = sb("t2", [128, 2, 512], F32)
        self.rope = sb("rope", [128, 2, 512], F32)
        self.pA = sb("pA", [128, 2, 512 + HL], F32)
        self.pT = sb("pT", [128, 4, 512], BF16)
        self.ao = sb("ao", [128, 2, 1024], BF16)
        self.xin = sb("xin", [128, 3, 2048], F32)
        self.rbf = sb("rbf", [128, 2, 512], BF16); self.rsq = sb("rsq", [128, 2, 512], BF16)
        self.st_mean = sb("st_mean", [128, 512], F32); self.st_rstd = sb("st_rstd", [128, 512], F32)
        self.st_tmp = sb("st_tmp", [128, 512], F32)
        self.tt = sb("tt", [128, 2, 512], F32)
        self.kf = self.st_tmp[:, :].rearrange("p (g t) -> p g t", g=4)
        self.identf = sb("identf", [128, 128], F32); self.identb = sb("identb", [128, 128], BF16)
        self.onesb = sb("onesb", [128, 128], BF16); self.permb = sb("permb", [128, 128], BF16)
        self.masks = sb("masks", [128, 3, 512], BF16)
        self.bvt = sb("bvt", [128, 256], F32)
        self.wp = sb("wp", [128, 2048], BF16)
        self.cv = sb("cv", [128, NCV], F32)
        self.sinkx = sb("sinkx", [128, 16], F32)
        self.den = sb("den", [128, 2, 4], F32)
        self.bst = sb("bst", [128, 3, 4, 6], F32); self.bmv = sb("bmv", [128, 3, 2], F32); self.brs = sb("brs", [128, 3, 2], F32)
        self.metaK = sb("metaK", [128, 4, 128], BF16); self.metaV = sb("metaV", [128, 4, 65], BF16)
        self.metaU = sb("metaU", [128, 8, 15], F32); self.uhist = sb("uhist", [128, 8, 15], F32)
        self.vst = sb("vst", [128, 2, 256], F32)
        self.ring = sb("ring", [128, R_SLOTS, UNIT], BF16)
        self.bank = [es.enter_context(nc.psum_tensor("bank%d" % i, [128, 512], F32)) for i in range(8)]

        self.ring_sem = [S.dma_sem("ring%d" % i) for i in range(R_SLOTS)]
        self.conv_sem = [S.dma_sem("conv%d" % i) for i in range(len(CONV_CHUNKS))]
        self.c_sem = S.dma_sem("consts")
        self.x_sem = [S.dma_sem("xin%d" % i) for i in range(3)]
        self.y_sem = [S.dma_sem("yout%d" % i) for i in range(3)]
        self.m_sem = S.dma_sem("misc_out"); self.v_sem = [S.dma_sem("vst0"), S.dma_sem("vst1")]
        self.rope_sem = S.dma_sem("rope")
        self.rr = {k: 0 for k in ("A", "G", "Bup", "den", "po2", "tt4", "z", "pt", "att", "po", "x", "ln", "tt", "ao", "pl")}

    XBUF = (0, 1, 0, 1)

    def x_prefetch(self, special, seq, j, b):
        S = self.S
        xb = self.XBUF[b]
        self.prefetched.add((special, seq, j, b))
        if special:
            src = self.xs[:, :]
        else:
            r0 = seq * SEQ + j * 512 + b * 128
            src = self.xp[r0:r0 + 128, :]
        S.op("sp", lambda h: h.dma_start(out=self.xin[:, xb, :], in_=src), writes=[("xin", xb)], dsem=self.x_sem[xb])
        for c4 in range(4):
            S.op("dve", lambda h, c4=c4: h.bn_stats(out=self.bst[:, xb, c4, :], in_=self.xin[:, xb, c4 * 512:(c4 + 1) * 512]),
                 reads=[("xin", xb)], writes=[("bst", xb)])
        S.op("dve", lambda h: h.bn_aggr(out=self.bmv[:, xb, :], in_=self.bst[:, xb, :, :]), reads=[("bst", xb)], writes=[("bmv", xb)])
        S.op("dve", lambda h: h.tensor_scalar(out=self.brs[:, xb, 0:1], in0=self.bmv[:, xb, 1:2], scalar1=EPS, scalar2=None, op0=ALU.add),
             reads=[("bmv", xb)], writes=[("brs", xb)])
        S.op("act", lambda h: h.sqrt(out=self.brs[:, xb, 0:1], in_=self.brs[:, xb, 0:1]), reads=[("brs", xb)], writes=[("brs", xb)])
        S.op("dve", lambda h: h.reciprocal(out=self.brs[:, xb, 0:1], in_=self.brs[:, xb, 0:1]), reads=[("brs", xb)], writes=[("brs", xb)])
        S.op("dve", lambda h: h.scalar_tensor_tensor(out=self.brs[:, xb, 1:2], in0=self.bmv[:, xb, 0:1], scalar=-1.0, in1=self.brs[:, xb, 0:1],
                                                     op0=ALU.mult, op1=ALU.mult),
             reads=[("bmv", xb), ("brs", xb)], writes=[("brs", xb)])
        S.op("act", lambda h: h.activation(out=self.xin[:, xb, :], in_=self.xin[:, xb, :], func=AF.Identity,
                                           bias=self.brs[:, xb, 1:2], scale=self.brs[:, xb, 0:1]),
             reads=[("brs", xb), ("xin", xb)], writes=[("xin", xb)])

    def views(self, special):
        U = self.U
        if special:
            n = 128
            aT = U[:, 0:NFC * n].rearrange("p (k t) -> p k t", t=n)
            mixT = U[:, 0:16 * n].rearrange("p (k t) -> p k t", t=n)
            o = 16 * n
            uT = U[:, o:o + 2 * 8 * (n + HL)].bitcast(F32).rearrange("p (k t) -> p k t", t=n + HL)
            o += 2 * 8 * (n + HL)
            dT = U[:, o:o + 8 * n].rearrange("p (k t) -> p k t", t=n)
            assert o + 8 * n <= NFC * n
        else:
            aT = U[:, :].rearrange("p (k t) -> p k t", t=512)
            mixT = U[:, 0:8192].rearrange("p (k t) -> p k t", t=512)
            uT = U[:, 8192:8192 + 2 * 8 * (512 + HL)].bitcast(F32).rearrange("p (k t) -> p k t", t=512 + HL)
            o = 8192 + 2 * 8 * (512 + HL)
            dT = U[:, o:o + 4096].rearrange("p (k t) -> p k t", t=512)
        return aT, mixT, uT, dT

    def rot(self, key, n=2):
        v = self.rr[key]; self.rr[key] = (v + 1) % n
        return v

    def prologue(self):
        S = self.S
        S.op("pool", lambda h: h.dma_start(out=self.wp[:], in_=self.wpool[:, :]), writes=["wp"], dsem=self.c_sem)
        for ci, (a, b) in enumerate(CONV_CHUNKS):
            for a2 in range(a, b, 4):
                b2 = min(b, a2 + 4)
                src = self.wt[a2:b2].rearrange("u p e -> (u p) e"); dst = self.wsc[a2:b2].rearrange("u p e -> (u p) e")
                S.op("pool", lambda h, s=src, d=dst: h.dma_start(out=d, in_=s), writes=[("wsc", ci)], dsem=self.conv_sem[ci])
        cs = self.c_sem
        ld = lambda o, i, w: S.op("sp", lambda h: h.dma_start(out=o, in_=i), writes=[w], dsem=cs)
        ld(self.identf[:], self.c_ident[:, :], "identf")
        ld(self.cv[:], self.colvec[:, :], "cv")
        ld(self.bvt[:], self.bv.partition_broadcast(128), "bvt")
        ld(self.sinkx[:], self.sinks.partition_broadcast(128), "sinkx")
        S.op("dve", lambda h: h.memset(self.onesb[:], 1.0), writes=["onesb"])
        S.op("sp", lambda h: h.dma_start(out=self.tt[:, 0, 0:128], in_=self.c_perm[:, :]), writes=[("tt", 0)], dsem=cs)
        S.op("dve", lambda h: h.tensor_copy(out=self.permb[:], in_=self.tt[:, 0, 0:128]), reads=[("tt", 0)], writes=["permb"])
        for m in range(3):
            S.op("sp", lambda h, m=m: h.dma_start(out=self.tt[:, 0, :], in_=self.c_masks[m]), writes=[("tt", 0)], dsem=cs)
            S.op("dve", lambda h, m=m: h.tensor_copy(out=self.masks[:, m, :], in_=self.tt[:, 0, :]), reads=[("tt", 0)], writes=["masks"])
        tot = (cs, S.dcnt[id(cs)])
        for rname in ("identf", "cv", "bvt", "sinkx", "wp"):
            S.lastw[rname] = tot
        S.op("act", lambda h: h.activation(out=self.sinkx[:], in_=self.sinkx[:], func=AF.Exp), reads=["sinkx"], writes=["sinkx"])
        S.op("dve", lambda h: h.tensor_copy(out=self.identb[:], in_=self.identf[:]), reads=["identf"], writes=["identb"])
        S.op("dve", lambda h: h.memset(self.Vx[:], 1.0), writes=["Vx"])
        S.op("dve", lambda h: h.memset(self.kT[:], 0.0), writes=["kT"])
        S.op("sp", lambda h: h.dma_start(out=self.nks[:, 0:127, :], in_=self.ck[:, 1:128, :]), dsem=self.m_sem)
        S.op("sp", lambda h: h.dma_start(out=self.nvs[:, 0:127, :], in_=self.cvv[:, 1:128, :]), dsem=self.m_sem)
        S.op("sp", lambda h: h.dma_start(out=self.nps[:, 0:14, :], in_=self.spool[:, 1:15, :]), dsem=self.m_sem)
        self.ring_next = 0
        self.ring_use = 0
        self.plan = list(META_UNITS) + list(range(NUNIT)) * (self.n_reg_tiles + 1)
        self.total_units = len(self.plan)
        for s in range(R_SLOTS):
            self.prefetch(s)

    def prefetch(self, slot):
        if self.ring_next >= self.total_units:
            return
        g = self.ring_next; self.ring_next += 1
        u = self.plan[g]
        kind, idx, nk = UNITS[u]
        n = nk * 128 if kind != "v" else 2048
        ci = [i for i, (a, b) in enumerate(CONV_CHUNKS) if a <= u < b][0]
        self.S.op("sp", lambda h, u=u, n=n, slot=slot: h.dma_start(out=self.ring[:, slot, 0:n], in_=self.wsc[u, :, 0:n]),
                  reads=[("wsc", ci)], writes=[("ring", slot)], dsem=self.ring_sem[slot])

    def take(self, kind):
        g = self.ring_use; self.ring_use += 1
        assert UNITS[self.plan[g]][0] == kind, (UNITS[self.plan[g]], kind)
        return g % R_SLOTS

    def mm_S(self, kind, bank, rhs_of_kc, N, nk=16, kc0=0, start=True, stop=True):
        slot = self.take(kind)
        ring = self.ring; bk = self.bank[bank]

        def fn(h):
            ins = None
            for kc in range(nk):
                ins = h.matmul(bk[:, 0:N], ring[:, slot, kc * 128:(kc + 1) * 128], rhs_of_kc(kc0 + kc),
                               start=(start and kc == 0), stop=(stop and kc == nk - 1))
            return ins
        return slot, fn

    def ln_fm(self, N, gname, bname, src_res, evac_fn_list, tag):
        S = self.S
        b1, b2 = 4, 5
        for dc in range(16):
            i = self.rot("ln")
            S.op("act", lambda h, dc=dc, i=i: h.copy(out=self.rbf[:, i, 0:N], in_=self.resT[:, dc, 0:N]),
                 reads=[("res", dc)], writes=[("rbf", i)])
            S.op("act", lambda h, dc=dc, i=i: h.activation(out=self.rsq[:, i, 0:N], in_=self.resT[:, dc, 0:N], func=AF.Square),
                 reads=[("res", dc)], writes=[("rsq", i)])
            S.op("pe", lambda h, dc=dc, i=i: h.matmul(self.bank[b1][:, 0:N], self.onesb[:], self.rbf[:, i, 0:N],
                                                     start=(dc == 0), stop=(dc == 15)),
                 reads=[("rbf", i), "onesb"], writes=[("bank", b1)])
            S.op("pe", lambda h, dc=dc, i=i: h.matmul(self.bank[b2][:, 0:N], self.onesb[:], self.rsq[:, i, 0:N],
                                                     start=(dc == 0), stop=(dc == 15)),
                 reads=[("rsq", i), "onesb"], writes=[("bank", b2)])
        mean, rstd, tmp = self.st_mean, self.st_rstd, self.st_tmp
        S.op("dve", lambda h: h.tensor_scalar(out=mean[:, 0:N], in0=self.bank[b1][:, 0:N], scalar1=1.0 / D, scalar2=None, op0=ALU.mult),
             reads=[("bank", b1)], writes=["st_mean"])
        S.op("dve", lambda h: h.tensor_tensor(out=tmp[:, 0:N], in0=mean[:, 0:N], in1=mean[:, 0:N], op=ALU.mult),
             reads=["st_mean"], writes=["st_tmp"])
        S.op("dve", lambda h: h.scalar_tensor_tensor(out=rstd[:, 0:N], in0=self.bank[b2][:, 0:N], scalar=1.0 / D, in1=tmp[:, 0:N],
                                                     op0=ALU.mult, op1=ALU.subtract),
             reads=[("bank", b2), "st_tmp"], writes=["st_rstd"])
        S.op("dve", lambda h: h.tensor_scalar(out=rstd[:, 0:N], in0=rstd[:, 0:N], scalar1=EPS, scalar2=None, op0=ALU.add),
             reads=["st_rstd"], writes=["st_rstd"])
        S.op("act", lambda h: h.sqrt(out=rstd[:, 0:N], in_=rstd[:, 0:N]), reads=["st_rstd"], writes=["st_rstd"])
        S.op("dve", lambda h: h.reciprocal(out=rstd[:, 0:N], in_=rstd[:, 0:N]), reads=["st_rstd"], writes=["st_rstd"])
        g0, b0 = CV[gname], CV[bname]
        pool_ok = self.cur_tile >= 2 or self.cur_tile == 0
        bufs = []
        for dc in range(16):
            i4 = self.rot("tt4", 4)
            tb, rn = ((self.tt, ("tt", i4)) if i4 < 2 else (self.t2, ("t2", i4 - 2)))
            bufs.append((tb[:, i4 % 2, 0:N], rn))

        def e_sub(dc):
            tv, rn = bufs[dc]
            S.op("dve", lambda h: h.tensor_tensor(out=tv, in0=self.resT[:, dc, 0:N], in1=mean[:, 0:N], op=ALU.subtract),
                 reads=[("res", dc), "st_mean"], writes=[rn])

        def e_rest(dc):
            tv, rn = bufs[dc]
            S.op("dve", lambda h: h.tensor_tensor(out=tv, in0=tv, in1=rstd[:, 0:N], op=ALU.mult),
                 reads=[rn, "st_rstd"], writes=[rn])
            if tag == "ln2":
                S.op("act", lambda h: h.activation(out=self.resT[:, dc, 0:N], in_=tv, func=AF.Identity,
                                                   bias=self.cv[:, b0 + dc:b0 + dc + 1], scale=self.cv[:, g0 + dc:g0 + dc + 1]),
                     reads=[rn, "cv"], writes=[("res", dc)])
            else:
                S.op("act", lambda h: h.activation(out=self.hT[:, dc, 0:N], in_=tv, func=AF.Identity,
                                                   bias=self.cv[:, b0 + dc:b0 + dc + 1], scale=self.cv[:, g0 + dc:g0 + dc + 1]),
                     reads=[rn, "cv"], writes=[("hT", dc)])
        e_sub(0)
        for dc in range(16):
            if dc + 1 < 16:
                e_sub(dc + 1)
            e_rest(dc)
        if tag != "ln2":
            de = "dve"
            for dc in range(16):
                pi = self.rot("pl")
                pv = self.pA[:, pi, 0:N]
                S.op(de, lambda h, dc=dc, pv=pv: h.tensor_tensor(out=pv, in0=self.resT[:, dc, 0:N], in1=mean[:, 0:N], op=ALU.subtract),
                     reads=[("res", dc), "st_mean"], writes=[("pA", pi)])
                S.op(de, lambda h, pv=pv: h.tensor_tensor(out=pv, in0=pv, in1=rstd[:, 0:N], op=ALU.mult),
                     reads=[("pA", pi), "st_rstd"], writes=[("pA", pi)])
                S.op("act", lambda h, dc=dc, pv=pv: h.activation(out=self.resT[:, dc, 0:N], in_=pv, func=AF.Identity,
                                                                bias=self.cv[:, b0 + dc:b0 + dc + 1], scale=self.cv[:, g0 + dc:g0 + dc + 1]),
                     reads=[("pA", pi), "cv"], writes=[("res", dc)])

    def run_tiles(self):
        self.cur_tile = -1
        self.prefetched = set()
        self.tile(special=True, seq=None, j=None, meta_only=True)
        for t in range(self.n_reg_tiles):
            self.cur_tile = t + 1
            self.tile(special=False, seq=t // 4, j=t % 4)
        self.cur_tile = 0
        self.tile(special=True, seq=None, j=None)

    def tile(self, special, seq, j, meta_only=False):
        S = self.S
        nblk = 1 if special else 4
        N = 128 * nblk
        first = (not special) and j == 0
        last = (not special) and j == 3
        aT, mixT, uT, dT = self.views(special)
        self.v_aT, self.v_mixT, self.v_uT, self.v_dT = aT, mixT, uT, dT
        ti = 4 if special else j
        S.op("sp", lambda h: h.dma_start(out=self.rope[:, 0, :], in_=self.c_rope[ti, 0]), writes=["rope"], dsem=self.rope_sem)
        S.op("sp", lambda h: h.dma_start(out=self.rope[:, 1, :], in_=self.c_rope[ti, 1]), writes=["rope"], dsem=self.rope_sem)
        if first:
            S.op("dve", lambda h: h.tensor_copy(out=self.kT[:, :, 0:128], in_=self.metaK[:]), reads=["metaK"], writes=["kT"])
            S.op("dve", lambda h: h.tensor_copy(out=self.Vx[:, 0], in_=self.metaV[:]), reads=["metaV"], writes=["Vx"])
        g0c, b0c = CV["g0"], CV["b0"]
        for b in range(nblk):
            xb = self.XBUF[b]
            if special or (special, seq, j, b) not in self.prefetched:
                self.x_prefetch(special, seq, j, b)
            for q4 in range(4):
                bk = self.rot("A", 4)

                def fn(h, q4=q4, xb=xb, bk=bk):
                    ins = None
                    for i in range(4):
                        dc = q4 * 4 + i
                        ins = h.transpose(out=self.bank[bk][:, i * 128:(i + 1) * 128], in_=self.xin[:, xb, dc * 128:(dc + 1) * 128],
                                          identity=self.identf[:])
                    return ins
                S.op("pe", fn, reads=[("xin", xb), "identf"], writes=[("bank", bk)])

                def fev(h, q4=q4, b=b, bk=bk):
                    ins = None
                    for i in range(4):
                        dc = q4 * 4 + i
                        ins = h.activation(out=self.resT[:, dc, b * 128:(b + 1) * 128], in_=self.bank[bk][:, i * 128:(i + 1) * 128], func=AF.Identity,
                                           bias=self.cv[:, b0c + dc:b0c + dc + 1], scale=self.cv[:, g0c + dc:g0c + dc + 1])
                    return ins
                if q4 % 2 == 0:
                    S.op("act", fev, reads=[("bank", bk), "cv"], writes=[("res", q4 * 4 + i) for i in range(4)])
                else:
                    ro = self.resT[:, q4 * 4:q4 * 4 + 4, b * 128:(b + 1) * 128]
                    gb = self.cv[:, g0c + q4 * 4:g0c + q4 * 4 + 4].unsqueeze(2).to_broadcast([128, 4, 128])
                    bb = self.cv[:, b0c + q4 * 4:b0c + q4 * 4 + 4].unsqueeze(2).to_broadcast([128, 4, 128])
                    S.op("dve", lambda h, ro=ro, gb=gb, bk=bk: h.tensor_tensor(out=ro, in0=self.bank[bk][:, :].rearrange("p (c t) -> p c t", t=128), in1=gb, op=ALU.mult),
                         reads=[("bank", bk), "cv"], writes=[("res", q4 * 4 + i) for i in range(4)])
                    S.op("dve", lambda h, ro=ro, bb=bb: h.tensor_tensor(out=ro, in0=ro, in1=bb, op=ALU.add),
                         reads=[("res", q4 * 4 + i) for i in range(4)] + ["cv"], writes=[("res", q4 * 4 + i) for i in range(4)])
                ce = "dve" if q4 % 2 == 0 else "act"
                S.op(ce, lambda h, q4=q4, b=b, ce=ce: (h.tensor_copy if ce == "dve" else h.copy)(
                    out=self.hT[:, q4 * 4:q4 * 4 + 4, b * 128:(b + 1) * 128], in_=self.resT[:, q4 * 4:q4 * 4 + 4, b * 128:(b + 1) * 128]),
                    reads=[("res", q4 * 4 + i) for i in range(4)], writes=[("hT", q4 * 4 + i) for i in range(4)])
        if self.stage <= 2 and self.cur_tile == self.n_reg_tiles:
            return self.dump(2, N)
        rhs_h = lambda kc: self.hT[:, kc, 0:N]
        def rope_post(kind, c, zi):
            pb = 6 + (zi % 2)
            S.op("pe", lambda h, zi=zi, pb=pb: h.matmul(self.bank[pb][:, 0:N], self.permb[:], self.zb[:, zi, 0:N], start=True, stop=True),
                 reads=[("zb", zi), "permb"], writes=[("bank", pb)])
            S.op("dve", lambda h, zi=zi, pb=pb: h.tensor_tensor(out=self.t2[:, zi, 0:N], in0=self.bank[pb][:, 0:N], in1=self.rope[:, 1, 0:N], op=ALU.mult),
                 reads=[("bank", pb), "rope"], writes=[("t2", zi)])
            S.op("dve", lambda h, zi=zi: h.tensor_tensor(out=self.zf[:, zi, 0:N], in0=self.zf[:, zi, 0:N], in1=self.rope[:, 0, 0:N], op=ALU.mult),
                 reads=[("zf", zi), "rope"], writes=[("zf", zi)])
            if kind == "q":
                S.op("dve", lambda h, zi=zi, c=c: h.tensor_tensor(out=self.qT[:, c, 0:N], in0=self.zf[:, zi, 0:N], in1=self.t2[:, zi, 0:N], op=ALU.add),
                     reads=[("zf", zi), ("t2", zi)], writes=[("qT", c)])
            else:
                S.op("dve", lambda h, zi=zi, c=c: h.tensor_tensor(out=self.kT[:, c, 128:128 + N], in0=self.zf[:, zi, 0:N], in1=self.t2[:, zi, 0:N], op=ALU.add),
                     reads=[("zf", zi), ("t2", zi)], writes=["kT"])
                if last or (special and not meta_only):
                    S.op("dve", lambda h, zi=zi, c=c: h.tensor_tensor(out=self.kf[:, c, :], in0=self.zf[:, zi, N - 128:N], in1=self.t2[:, zi, N - 128:N], op=ALU.add),
                         reads=[("zf", zi), ("t2", zi)], writes=["st_tmp"])

        pend = None
        for kind, cnt, bcol in (("q", 8, CV["bq"]), ("k", 4, CV["bk"])):
            if meta_only and kind == "q":
                continue
            for c in range(cnt):
                bk = self.rot("A", 4)
                slot, fn = self.mm_S(kind, bk, rhs_h, N)
                S.op("pe", fn, reads=[("ring", slot)] + [("hT", dc) for dc in range(16)], writes=[("bank", bk)])
                self.prefetch(slot)
                zi = self.rot("z")
                S.op("act", lambda h, zi=zi, bk=bk, c=c, bcol=bcol: h.activation(out=self.zf[:, zi, 0:N], in_=self.bank[bk][:, 0:N], func=AF.Identity,
                                                                              bias=self.cv[:, bcol + c:bcol + c + 1], scale=1.0),
                     reads=[("bank", bk), "cv"], writes=[("zf", zi)])
                S.op("act", lambda h, zi=zi: h.copy(out=self.zb[:, zi, 0:N], in_=self.zf[:, zi, 0:N]), reads=[("zf", zi)], writes=[("zb", zi)])
                if pend is not None:
                    rope_post(*pend)
                pend = (kind, c, zi)
        rope_post(*pend)
        if last or (special and not meta_only):
            self.emit_new_k(seq, special)
        if self.stage <= 3 and self.cur_tile == self.n_reg_tiles:
            return self.dump(3, N)
        vslots = [self.take("v"), self.take("v")]
        vb = [self.rot("A", 4) for _ in range(nblk)]
        for hh in range(2):
            def fn(h, hh=hh):
                ins = None
                for b in range(nblk):
                    for kc in range(8):
                        ins = h.matmul(self.bank[vb[b]][:, 0:256], self.hT[:, hh * 8 + kc, b * 128:(b + 1) * 128],
                                       self.ring[:, vslots[hh], kc * 256:(kc + 1) * 256],
                                       start=(hh == 0 and kc == 0), stop=(hh == 1 and kc == 7))
                return ins
            S.op("pe", fn, reads=[("ring", vslots[hh])] + [("hT", dc) for dc in range(16)], writes=[("bank", x) for x in vb])
            self.prefetch(vslots[hh])
        for b in range(nblk):
            S.op("dve", lambda h, b=b: h.tensor_tensor(out=self.Vx[:, 1 + b, :, 0:64], in0=self.bank[vb[b]][:, 0:256].rearrange("p (g d) -> p g d", g=4),
                                                      in1=self.bvt[:].rearrange("p (g d) -> p g d", g=4), op=ALU.add),
                 reads=[("bank", vb[b]), "bvt"], writes=["Vx"])
            if (last and b == nblk - 1) or (special and not meta_only):
                S.op("dve", lambda h, b=b: h.tensor_tensor(out=self.vst[:, 0, :], in0=self.bank[vb[b]][:, 0:256], in1=self.bvt[:], op=ALU.add),
                     reads=[("bank", vb[b]), "bvt"], writes=["vst0"])
                if special:
                    S.op("sp", lambda h: h.dma_start(out=self.nvs[:, 127, :], in_=self.vst[0:NSAMP, 0, :]), reads=["vst0"], dsem=self.v_sem[0])
                else:
                    S.op("sp", lambda h: h.dma_start(out=self.nvp[seq], in_=self.vst[:, 0, :]), reads=["vst0"], dsem=self.v_sem[0])
        if self.stage <= 4 and self.cur_tile == self.n_reg_tiles:
            return self.dump(4, N)
        if special and not meta_only:
            self.sample_pool_prep()
        for f in range(8):
            bk = self.rot("A", 4)
            slot, fn = self.mm_S("u", bk, rhs_h, N)
            S.op("pe", fn, reads=[("ring", slot)] + [("hT", dc) for dc in range(16)], writes=[("bank", bk)])
            self.prefetch(slot)
            S.op("act", lambda h, f=f, bk=bk: h.activation(out=uT[:, f, HL:HL + N], in_=self.bank[bk][:, 0:N], func=AF.Identity,
                                                          bias=self.cv[:, CV["bu"] + f:CV["bu"] + f + 1], scale=1.0),
                 reads=[("bank", bk), "cv"], writes=["uT"])
            if meta_only:
                continue
            if special:
                S.op("dve", lambda h, f=f: h.memset(uT[:, f, 0:HL], 0.0), writes=["uT"])
            else:
                hsrc = self.metaU if first else self.uhist
                S.op("dve", lambda h, f=f, hsrc=hsrc: h.tensor_copy(out=uT[:, f, 1:HL], in_=hsrc[:, f, :]),
                     reads=["metaU", "uhist"], writes=["uT"])
            gi = f // 2
            L = HL + N
            srcap = uT[:, f, :]
            rd = ["uT"]
            for lvl in range(gi + 1):
                sh = 1 << lvl
                pi = self.rot("pl")
                dst = self.pA[:, pi, :]
                S.op("dve", lambda h, s=srcap, d=dst, sh=sh, L=L: h.tensor_tensor(out=d[:, sh:L], in0=s[:, sh:L], in1=s[:, 0:L - sh], op=ALU.add),
                     reads=rd, writes=[("pA", pi)])
                srcap = dst; rd = [("pA", pi)]
            w = 2 << gi
            S.op("dve", lambda h, s=srcap, f=f, w=w: h.scalar_tensor_tensor(out=dT[:, f, 0:N], in0=s[:, HL:HL + N], scalar=1.0 / w,
                                                                          in1=uT[:, f, HL:HL + N], op0=ALU.mult, op1=ALU.subtract),
                 reads=rd + ["uT"], writes=[("dT", f)])
            if special:
                ss = self.st_tmp
                S.op("dve", lambda h, f=f, w=w, ss=ss: h.tensor_reduce(out=ss[:, 0:NSAMP], in_=self.histT[:, f, :, 16 - w:15], axis=AX.X, op=ALU.add),
                     reads=["histT"], writes=["st_tmp"])
                S.op("dve", lambda h, f=f, ss=ss: h.tensor_tensor(out=ss[:, 0:NSAMP], in0=ss[:, 0:NSAMP], in1=uT[:, f, HL:HL + NSAMP], op=ALU.add),
                     reads=["st_tmp", "uT"], writes=["st_tmp"])
                S.op("dve", lambda h, f=f, w=w, ss=ss: h.scalar_tensor_tensor(out=dT[:, f, 0:NSAMP], in0=ss[:, 0:NSAMP], scalar=1.0 / w,
                                                                             in1=uT[:, f, HL:HL + NSAMP], op0=ALU.mult, op1=ALU.subtract),
                     reads=["st_tmp", "uT"], writes=[("dT", f)])
        if meta_only:
            S.op("dve", lambda h: h.tensor_copy(out=self.metaU[:], in_=uT[:, :, HL + 113:HL + 128]), reads=["uT"], writes=["metaU"])
            S.op("dve", lambda h: h.tensor_copy(out=self.metaK[:], in_=self.kT[:, :, 128:256]), reads=["kT"], writes=["metaK"])
            S.op("dve", lambda h: h.tensor_copy(out=self.metaV[:], in_=self.Vx[:, 1]), reads=["Vx"], writes=["metaV"])
            return
        if special:
            self.emit_new_pool(None, N, uT, sample=True)
        elif last:
            self.emit_new_pool(seq, N, uT)
        else:
            S.op("dve", lambda h: h.tensor_copy(out=self.uhist[:], in_=uT[:, :, N + 1:N + HL]), reads=["uT"], writes=["uhist"])
        if self.stage <= 5 and self.cur_tile == self.n_reg_tiles:
            return self.dump(5, N)
        for oc in range(8):
            gi = oc // 2; o2 = oc % 2
            bk = self.rot("A", 4)

            def fn(h, gi=gi, o2=o2, bk=bk):
                ins = None
                for kc in range(2):
                    c0 = gi * 512 + kc * 256 + o2 * 128
                    ins = h.matmul(self.bank[bk][:, 0:N], self.wp[:, c0:c0 + 128], dT[:, gi * 2 + kc, 0:N], start=(kc == 0), stop=(kc == 1))
                return ins
            S.op("pe", fn, reads=["wp", ("dT", gi * 2), ("dT", gi * 2 + 1)], writes=[("bank", bk)])
            S.op("dve", lambda h, oc=oc, bk=bk: h.tensor_scalar(out=mixT[:, 8 + oc, 0:N], in0=self.bank[bk][:, 0:N],
                                                               scalar1=self.cv[:, CV["ps"] + oc:CV["ps"] + oc + 1], scalar2=None, op0=ALU.mult),
                 reads=[("bank", bk), "cv"], writes=[("mixT", 8 + oc)])
        if self.stage <= 6 and self.cur_tile == self.n_reg_tiles:
            return self.dump(6, N)
        if special:
            self.sample_cache_prep()
        mks = [2 if special else (1 if (first and b == 0) else 0) for b in range(nblk)]
        self.attention_tile(nblk, mks, mixT, special)
        if (not special) and (not last):
            S.op("dve", lambda h: h.tensor_copy(out=self.kT[:, :, 0:128], in_=self.kT[:, :, 512:640]), reads=["kT"], writes=["kT"])
            S.op("dve", lambda h: h.tensor_copy(out=self.Vx[:, 0], in_=self.Vx[:, 4]), reads=["Vx"], writes=["Vx"])
        if self.stage <= 7 and self.cur_tile == self.n_reg_tiles:
            return self.dump(7, N)
        rhs_m = lambda kc: mixT[:, kc, 0:N]
        for dc in range(16):
            bk = self.rot("A", 4)
            slot, fn = self.mm_S("o", bk, rhs_m, N)
            S.op("pe", fn, reads=[("ring", slot)] + [("mixT", k) for k in range(16)], writes=[("bank", bk)])
            self.prefetch(slot)
            S.op("dve", lambda h, dc=dc, bk=bk: h.scalar_tensor_tensor(out=self.resT[:, dc, 0:N], in0=self.resT[:, dc, 0:N], scalar=ALPHA,
                                                                      in1=self.bank[bk][:, 0:N], op0=ALU.mult, op1=ALU.add),
                 reads=[("bank", bk), ("res", dc)], writes=[("res", dc)])
        self.ln_fm(N, "g1", "b1", None, None, "ln1")
        if self.stage <= 8 and self.cur_tile == self.n_reg_tiles:
            return self.dump(8, N)
        for f in range(NFC):
            bg = self.rot("G", 2); bu = 2 + self.rot("Bup", 2)
            slot, fn = self.mm_S("g", bg, rhs_h, N)
            S.op("pe", fn, reads=[("ring", slot)] + [("hT", dc) for dc in range(16)], writes=[("bank", bg)])
            self.prefetch(slot)
            slot, fn = self.mm_S("up", bu, rhs_h, N)
            S.op("pe", fn, reads=[("ring", slot)] + [("hT", dc) for dc in range(16)], writes=[("bank", bu)])
            self.prefetch(slot)
            i = self.rot("tt")
            S.op("act", lambda h, i=i, bg=bg: h.activation(out=self.tt[:, i, 0:N], in_=self.bank[bg][:, 0:N], func=AF.Silu),
                 reads=[("bank", bg)], writes=[("tt", i)])
            S.op("dve", lambda h, i=i, bu=bu, f=f: h.tensor_tensor(out=aT[:, f, 0:N], in0=self.tt[:, i, 0:N], in1=self.bank[bu][:, 0:N], op=ALU.mult),
                 reads=[("tt", i), ("bank", bu)], writes=[("aT", f)])
        if self.stage <= 9 and self.cur_tile == self.n_reg_tiles:
            return self.dump(9, N)
        if (not special) and self.cur_tile < self.n_reg_tiles:
            nt_ = self.cur_tile
            for pb_ in range(2):
                self.x_prefetch(False, nt_ // 4, nt_ % 4, pb_)
        for dc in range(16):
            bk = self.rot("A", 4)
            for part in range(3):
                nk = 16 if part < 2 else 12
                slot, fn = self.mm_S("d", bk, lambda kc: aT[:, kc, 0:N], N, nk=nk, kc0=part * 16, start=(part == 0), stop=(part == 2))
                S.op("pe", fn, reads=[("ring", slot)] + [("aT", part * 16 + k) for k in range(nk)], writes=[("bank", bk)])
                self.prefetch(slot)
            S.op("dve", lambda h, dc=dc, bk=bk: h.scalar_tensor_tensor(out=self.resT[:, dc, 0:N], in0=self.resT[:, dc, 0:N], scalar=ALPHA,
                                                                      in1=self.bank[bk][:, 0:N], op0=ALU.mult, op1=ALU.add),
                 reads=[("bank", bk), ("res", dc)], writes=[("res", dc)])
        self.ln_fm(N, "g2", "b2", None, None, "ln2")
        if self.stage <= 10 and self.cur_tile == self.n_reg_tiles:
            return self.dump(10, N)
        for b in range(nblk):
            yb = 2
            for q4 in range(4):
                bk = self.rot("A", 4)

                def fn(h, q4=q4, b=b, bk=bk):
                    ins = None
                    for i in range(4):
                        dc = q4 * 4 + i
                        ins = h.transpose(out=self.bank[bk][:, i * 128:(i + 1) * 128], in_=self.resT[:, dc, b * 128:(b + 1) * 128],
                                          identity=self.identf[:])
                    return ins
                S.op("pe", fn, reads=[("res", q4 * 4 + i) for i in range(4)] + ["identf"], writes=[("bank", bk)])
                eng = "act" if q4 % 2 == 0 else "dve"
                if eng == "act":
                    S.op("act", lambda h, yb=yb, q4=q4, bk=bk: h.copy(out=self.xin[:, yb, q4 * 512:(q4 + 1) * 512], in_=self.bank[bk][:, :]),
                         reads=[("bank", bk)], writes=[("xin", yb)])
                else:
                    S.op("dve", lambda h, yb=yb, q4=q4, bk=bk: h.tensor_copy(out=self.xin[:, yb, q4 * 512:(q4 + 1) * 512], in_=self.bank[bk][:, :]),
                         reads=[("bank", bk)], writes=[("xin", yb)])
            for q4 in range(4):
                if special:
                    S.op("sp", lambda h, yb=yb, q4=q4: h.dma_start(out=self.ys[:, q4 * 512:(q4 + 1) * 512], in_=self.xin[0:NSAMP, yb, q4 * 512:(q4 + 1) * 512]), reads=[("xin", yb)], dsem=self.y_sem[yb])
                else:
                    r0 = seq * SEQ + j * 512 + b * 128
                    S.op("sp", lambda h, yb=yb, r0=r0, q4=q4: h.dma_start(out=self.yp[r0:r0 + 128, q4 * 512:(q4 + 1) * 512], in_=self.xin[:, yb, q4 * 512:(q4 + 1) * 512]), reads=[("xin", yb)], dsem=self.y_sem[yb])

    def dump(self, st, N):
        S = self.S
        if self.dbg is None or getattr(self, "dumped", False):
            return
        self.dumped = True
        if st in (1, 2, 8, 10):
            S.op("sp", lambda h: h.dma_start(out=self.dbg.rearrange("p (c t) -> p c t", t=512), in_=self.resT[:]),
                 reads=[("res", dc) for dc in range(16)], dsem=self.m_sem)
        elif st in (3, 4):
            S.op("dve", lambda h: h.tensor_copy(out=self.resT[:, 0:8, :], in_=self.qT[:]), reads=[("qT", c) for c in range(8)], writes=[("res", dc) for dc in range(16)])
            S.op("dve", lambda h: h.tensor_copy(out=self.resT[:, 8:12, :], in_=self.kT[:, :, 128:640]), reads=["kT"], writes=[("res", dc) for dc in range(16)])
            S.op("dve", lambda h: h.tensor_copy(out=self.resT[:, 12:14, 0:260], in_=self.Vx[:, 1:3].rearrange("p a g d -> p a (g d)")), reads=["Vx"], writes=[("res", dc) for dc in range(16)])
            S.op("sp", lambda h: h.dma_start(out=self.dbg.rearrange("p (c t) -> p c t", t=512), in_=self.resT[:]),
                 reads=[("res", dc) for dc in range(16)], dsem=self.m_sem)
        elif st in (5, 6, 7):
            S.op("dve", lambda h: h.tensor_copy(out=self.resT[:], in_=self.v_mixT[:]), reads=[("mixT", c) for c in range(16)] + [("dT", f) for f in range(8)], writes=[("res", dc) for dc in range(16)])
            S.op("sp", lambda h: h.dma_start(out=self.dbg.rearrange("p (c t) -> p c t", t=512), in_=self.resT[:]),
                 reads=[("res", dc) for dc in range(16)], dsem=self.m_sem)
        elif st == 9:
            S.op("dve", lambda h: h.tensor_copy(out=self.resT[:], in_=self.v_aT[:, 0:16, :]), reads=[("aT", f) for f in range(NFC)], writes=[("res", dc) for dc in range(16)])
            S.op("sp", lambda h: h.dma_start(out=self.dbg.rearrange("p (c t) -> p c t", t=512), in_=self.resT[:]),
                 reads=[("res", dc) for dc in range(16)], dsem=self.m_sem)

    def sample_pool_prep(self):
        S = self.S
        for half in range(2):
            xb = 1 + self.rot("x")
            S.op("sp", lambda h, xb=xb, half=half: h.dma_start(out=self.xin[0:120, xb, 0:1024],
                                                                in_=self.spool[half * 8:(half + 1) * 8].rearrange("b r c -> (b r) c")),
                 writes=[("xin", xb)], dsem=self.x_sem[xb])
            for q2 in range(2):
                bk = self.rot("A", 4)

                def fn(h, q2=q2, xb=xb, bk=bk):
                    ins = None
                    for i in range(4):
                        f = q2 * 4 + i
                        ins = h.transpose(out=self.bank[bk][:, i * 120:(i + 1) * 120], in_=self.xin[0:120, xb, f * 128:(f + 1) * 128],
                                          identity=self.identf[0:120, 0:120])
                    return ins
                S.op("pe", fn, reads=[("xin", xb), "identf"], writes=[("bank", bk)])
                S.op("dve", lambda h, q2=q2, half=half, bk=bk: h.tensor_copy(
                    out=self.histT[:, q2 * 4:q2 * 4 + 4, half * 8:(half + 1) * 8, :],
                    in_=self.bank[bk][:, 0:480].rearrange("p (f b r) -> p f b r", f=4, b=8)),
                    reads=[("bank", bk)], writes=["histT"])

    def sample_cache_prep(self):
        S = self.S
        S.op("dve", lambda h: h.memset(self.cVe[:], 1.0), reads=["histT"], writes=["cVe"])
        S.op("dve", lambda h: h.memset(self.Pexp[:], 0.0), reads=["histT"], writes=["Pexp", "histT"])
        for r in range(2):
            xb = 1 + self.rot("x")
            S.op("sp", lambda h, xb=xb, r=r: h.dma_start(out=self.xin[:, xb, :].rearrange("p (b c) -> p b c", b=8),
                                                          in_=self.cvv[r * 8:(r + 1) * 8].rearrange("b k c -> k b c")),
                 writes=[("xin", xb)], dsem=self.x_sem[xb])
            S.op("dve", lambda h, xb=xb, r=r: h.tensor_copy(out=self.cVe[:, r * 8:(r + 1) * 8, :, 0:64],
                                                            in_=self.xin[:, xb, :].rearrange("p (b g d) -> p b g d", b=8, g=4)),
                 reads=[("xin", xb)], writes=["cVe"])
        aof = self.ao[:].rearrange("p a n -> p (a n)")
        ckd = aof.rearrange("p (b g t d) -> p b g t d", b=4, g=4, t=2)
        for r in range(4):
            xb = 1 + self.rot("x")
            S.op("sp", lambda h, xb=xb, r=r: h.dma_start(out=self.xin[:, xb, 0:1024].rearrange("p (b c) -> p b c", b=4),
                                                          in_=self.ck[r * 4:(r + 1) * 4].rearrange("b k c -> k b c")),
                 writes=[("xin", xb)], dsem=self.x_sem[xb])
            for t in range(2):
                S.op("dve", lambda h, xb=xb, t=t: h.tensor_copy(out=ckd[:, :, :, t, :],
                                                                in_=self.xin[:, xb, 0:1024].rearrange("p (b g d) -> p b g d", b=4, g=4)),
                     reads=[("xin", xb)], writes=[("ao", 0), ("ao", 1)])
            for bl in range(4):
                bk = self.rot("A", 4)

                def fn(h, bl=bl, bk=bk):
                    ins = None
                    for g in range(4):
                        ins = h.matmul(self.bank[bk][:, g * 128:(g + 1) * 128], ckd[:, bl, g].rearrange("p t d -> p (t d)"), self.identb[:],
                                       start=True, stop=True)
                    return ins
                S.op("pe", fn, reads=[("ao", 0), ("ao", 1), "identb"], writes=[("bank", bk)])
                S.op("act", lambda h, r=r, bl=bl, bk=bk: h.copy(out=self.cKT[:, r * 4 + bl, :, :],
                                                               in_=self.bank[bk][:, :].rearrange("p (g k) -> p g k", g=4)),
                     reads=[("bank", bk)], writes=["cKT"])

    def attention_tile(self, nblk, mks, mixT, special=False):
        S = self.S
        prev = None
        ais = [None] * nblk
        for b in range(nblk):
            ais[b] = self.rot("ao")
            for g in range(4):
                st = self.att_stage1(b, g, mks[b], special)
                if prev is not None:
                    self.att_stage2(prev, ais, mixT, special)
                prev = st
        self.att_stage2(prev, ais, mixT, special)

    def att_stage1(self, b, g, mk, special):
        S = self.S
        qc = slice(b * 128, (b + 1) * 128)
        prevc = slice(b * 128, (b + 1) * 128)
        curc = slice((b + 1) * 128, (b + 2) * 128)
        r2 = self.rot("att")
        sbk = (r2, 6 + r2)

        def fqk(h):
            ins = None
            for pr in range(2):
                c = 2 * g + pr
                for hh in range(2):
                    ps = slice(hh * 64, hh * 64 + 64)
                    st = self.bank[sbk[hh]]
                    if special:
                        for sb in range(NSAMP):
                            h.matmul(st[:, pr * 256 + sb:pr * 256 + sb + 1], self.cKT[ps, sb, g, :], self.qT[ps, c, sb:sb + 1], start=True, stop=True)
                    else:
                        h.matmul(st[:, pr * 256:pr * 256 + 128], self.kT[ps, g, prevc], self.qT[ps, c, qc], start=True, stop=True)
                    ins = h.matmul(st[:, pr * 256 + 128:pr * 256 + 256], self.kT[ps, g, curc], self.qT[ps, c, qc], start=True, stop=True)
            return ins
        S.op("pe", fqk, reads=["kT", "cKT", ("qT", 2 * g), ("qT", 2 * g + 1)], writes=[("bank", sbk[0]), ("bank", sbk[1])])
        pis = []
        for hh in range(2):
            st = self.bank[sbk[hh]]
            pi = self.rot("pt", 4)
            pis.append(pi)
            S.op("act", lambda h, pi=pi, st=st: h.activation(out=self.pT[:, pi, :], in_=st[:, :], func=AF.Exp, scale=0.125),
                 reads=[("bank", sbk[hh])], writes=[("pT", pi)])
            S.op("dve", lambda h, pi=pi: h.tensor_tensor(out=self.pT[:, pi, :], in0=self.pT[:, pi, :], in1=self.masks[:, mk, :], op=ALU.mult),
                 reads=[("pT", pi), "masks"], writes=[("pT", pi)])
        return (b, g, pis)

    def att_stage2(self, st, ais, mixT, special):
        S = self.S
        b, g, pis = st
        ai = ais[b]
        qc = slice(b * 128, (b + 1) * 128)
        pob = 4 + self.rot("po")
        po2b = 2 + self.rot("po2")
        for hh in range(2):
            pi = pis[hh]

            def fpv(h, pi=pi, hh=hh):
                ins = None
                for pr in range(2):
                    o = self.bank[pob][:, (pr * 2 + hh) * 65:(pr * 2 + hh) * 65 + 65]
                    if not special:
                        h.matmul(o, self.pT[:, pi, pr * 256:pr * 256 + 128], self.Vx[:, b, g, :], start=True, stop=False)
                    ins = h.matmul(o, self.pT[:, pi, pr * 256 + 128:pr * 256 + 256], self.Vx[:, b + 1, g, :], start=special, stop=True)
                return ins
            S.op("pe", fpv, reads=[("pT", pi), "Vx"], writes=[("bank", pob)])
            if special:
                for pr in range(2):
                    hd = 4 * g + pr * 2 + hh
                    S.op("dve", lambda h, pi=pi, pr=pr, hd=hd: h.tensor_copy(
                        out=self.Pexp[:, hd].rearrange("p b c -> p (b c)")[:, 0:256:17], in_=self.pT[:, pi, pr * 256:pr * 256 + NSAMP]),
                        reads=[("pT", pi)], writes=["Pexp"])

                def fpv2(h, hh=hh):
                    ins = None
                    for pr in range(2):
                        hd = 4 * g + pr * 2 + hh
                        o = self.bank[po2b][0:NSAMP, (pr * 2 + hh) * 65:(pr * 2 + hh) * 65 + 65]
                        for sb in range(NSAMP):
                            ins = h.matmul(o, self.Pexp[:, hd, sb, :], self.cVe[:, sb, g, :], start=(sb == 0), stop=(sb == NSAMP - 1))
                    return ins
                S.op("pe", fpv2, reads=["Pexp", "cVe"], writes=[("bank", po2b)])
        po4 = self.bank[pob][:, 0:260].rearrange("p (h d) -> p h d", h=4)
        di = self.rot("den")
        S.op("dve", lambda h: h.tensor_tensor(out=self.den[:, di, :].unsqueeze(2), in0=po4[:, :, 64:65],
                                              in1=self.sinkx[:, 4 * g:4 * g + 4].unsqueeze(2), op=ALU.add),
             reads=[("bank", pob), "sinkx"], writes=[("den", di)])
        S.op("dve", lambda h: h.reciprocal(out=self.den[:, di, :], in_=self.den[:, di, :]), reads=[("den", di)], writes=[("den", di)])
        S.op("dve", lambda h: h.tensor_tensor(
            out=self.ao[:, ai, g * 256:(g + 1) * 256].rearrange("p (h d) -> p h d", h=4), in0=po4[:, :, 0:64],
            in1=self.den[:, di, :].unsqueeze(2).to_broadcast([128, 4, 64]), op=ALU.mult),
            reads=[("bank", pob), ("den", di)], writes=[("ao", ai)])
        if special:
            tm = self.st_tmp
            tm4 = tm[0:NSAMP, 0:260].rearrange("p (h d) -> p h d", h=4)
            S.op("act", lambda h: h.copy(out=tm[0:NSAMP, 0:260], in_=self.bank[po2b][0:NSAMP, 0:260]),
                 reads=[("bank", po2b)], writes=["st_tmp"])
            S.op("dve", lambda h: h.tensor_tensor(out=tm[0:NSAMP, 0:260], in0=tm[0:NSAMP, 0:260], in1=self.bank[pob][0:NSAMP, 0:260], op=ALU.add),
                 reads=["st_tmp", ("bank", pob)], writes=["st_tmp"])
            S.op("dve", lambda h: h.tensor_tensor(out=self.den[0:NSAMP, di, :].unsqueeze(2), in0=tm4[:, :, 64:65],
                                                  in1=self.sinkx[0:NSAMP, 4 * g:4 * g + 4].unsqueeze(2), op=ALU.add),
                 reads=["st_tmp", "sinkx"], writes=[("den", di)])
            S.op("dve", lambda h: h.reciprocal(out=self.den[0:NSAMP, di, :], in_=self.den[0:NSAMP, di, :]), reads=[("den", di)], writes=[("den", di)])
            S.op("dve", lambda h: h.tensor_tensor(
                out=self.ao[0:NSAMP, ai, g * 256:(g + 1) * 256].rearrange("p (h d) -> p h d", h=4), in0=tm4[:, :, 0:64],
                in1=self.den[0:NSAMP, di, :].unsqueeze(2).to_broadcast([NSAMP, 4, 64]), op=ALU.mult),
                reads=["st_tmp", ("den", di)], writes=[("ao", ai)])
        if g == 3:
            bk = self.rot("A", 4)
            tpv = self.bank[bk][:, :].bitcast(BF16)

            def ftr(h):
                ins = None
                for c in range(8):
                    ins = h.transpose(out=tpv[:, c * 128:(c + 1) * 128], in_=self.ao[:, ai, c * 128:(c + 1) * 128], identity=self.identb[:])
                return ins
            S.op("pe", ftr, reads=[("ao", ai), "identb"], writes=[("bank", bk)])
            S.op("act", lambda h: h.copy(out=mixT[:, 0:8, qc], in_=tpv.rearrange("p (c t) -> p c t", t=128)),
                 reads=[("bank", bk)], writes=[("mixT", k) for k in range(8)])

    def emit_new_pool(self, seq, N, uT, sample=False):
        S = self.S
        yb = 2
        c0, c1, nr = (HL, HL + NSAMP, NSAMP) if sample else (N + 1, N + HL, 15)
        for half in range(2):
            bk = self.rot("A", 4)

            def fn(h, half=half, bk=bk):
                ins = None
                for i in range(4):
                    f = half * 4 + i
                    ins = h.transpose(out=self.bank[bk][0:nr, i * 128:(i + 1) * 128], in_=uT[:, f, c0:c1], identity=self.identf[:])
                return ins
            S.op("pe", fn, reads=["uT", "identf"], writes=[("bank", bk)])
            S.op("dve", lambda h, half=half, bk=bk, yb=yb: h.tensor_copy(out=self.xin[0:nr, yb, half * 512:(half + 1) * 512], in_=self.bank[bk][0:nr, :]),
                 reads=[("bank", bk)], writes=[("xin", yb)])
        if sample:
            S.op("sp", lambda h, yb=yb: h.dma_start(out=self.nps[:, 14, :], in_=self.xin[0:NSAMP, yb, 0:1024]), reads=[("xin", yb)], dsem=self.y_sem[yb])
        else:
            S.op("sp", lambda h, yb=yb: h.dma_start(out=self.npp[seq], in_=self.xin[0:15, yb, 0:1024]), reads=[("xin", yb)], dsem=self.y_sem[yb])

    def emit_new_k(self, seq, special=False):
        S = self.S
        bk = self.rot("A", 4)

        def fn(h, bk=bk):
            ins = None
            for g in range(4):
                ins = h.transpose(out=self.bank[bk][:, g * 64:(g + 1) * 64], in_=self.kf[0:64, g, :], identity=self.identf[0:64, 0:64])
            return ins
        S.op("pe", fn, reads=["st_tmp", "identf"], writes=[("bank", bk)])
        S.op("dve", lambda h, bk=bk: h.tensor_copy(out=self.vst[:, 1, :], in_=self.bank[bk][:, 0:256]), reads=[("bank", bk)], writes=["vst1"])
        if special:
            S.op("sp", lambda h: h.dma_start(out=self.nks[:, 127, :], in_=self.vst[0:NSAMP, 1, :]), reads=["vst1"], dsem=self.v_sem[1])
        else:
            S.op("sp", lambda h: h.dma_start(out=self.nkp[seq], in_=self.vst[:, 1, :]), reads=["vst1"], dsem=self.v_sem[1])

    def epilogue(self):
        S = self.S
        for s in self.y_sem + [self.m_sem] + self.v_sem:
            if S.dcnt[id(s)] > 0:
                S.wait_tok("sp", (s, S.dcnt[id(s)]))


_CACHE = {}


def _program(n_reg_tiles=8, stage=99):
    if (n_reg_tiles, stage) not in _CACHE:
        _CACHE[(n_reg_tiles, stage)] = Builder(n_reg_tiles, stage).build()
    return _CACHE[(n_reg_tiles, stage)]


def kernel(x_prompt, x_sample, cache_k, cache_v, state_pool, meta_tokens, ln_in_g, ln_in_b,
           w_in, b_in, attn_sinks, w_pool, pool_scale, w_o, ln1_g, ln1_b,
           w_gate, w_up, w_down, ln2_g, ln2_b, _n_reg_tiles=8, _stage=99, _dbg=None, _ncores=NCORE):
    f = lambda a: np.ascontiguousarray(np.asarray(a, dtype=np.float32))
    x_prompt = f(x_prompt); x_sample = f(x_sample); cache_k = f(cache_k); cache_v = f(cache_v); state_pool = f(state_pool)
    wt = host_weights(f(w_in)[0], f(w_o)[0], f(w_gate)[0], f(w_up)[0], f(w_down)[0])
    wpool = np.ascontiguousarray(f(w_pool)[0].reshape(4, 2, 128, 256).transpose(2, 0, 1, 3).reshape(128, 2048))
    colvec = host_colvec(f(b_in)[0], f(pool_scale)[0], f(ln_in_g), f(ln_in_b), f(ln1_g)[0], f(ln1_b)[0], f(ln2_g)[0], f(ln2_b)[0])
    bv = np.ascontiguousarray(f(b_in)[0][1280:1536]); sinks = np.ascontiguousarray(f(attn_sinks)[0])
    ident, perm, masks, rope = host_consts()
    meta = f(meta_tokens)
    in_maps = []
    for c in range(NCORE):
        xs = np.zeros((128, D), np.float32)
        xs[0:NSAMP] = x_sample[c * NSAMP:(c + 1) * NSAMP, 0]
        xs[112:128] = meta
        in_maps.append({
            "xp": x_prompt[2 * c:2 * c + 2].reshape(2 * SEQ, D), "xs": xs,
            "ck": cache_k[0, c * NSAMP:(c + 1) * NSAMP].reshape(NSAMP, 128, 256),
            "cvv": cache_v[0, c * NSAMP:(c + 1) * NSAMP].reshape(NSAMP, 128, 256),
            "spool": state_pool[0, c * NSAMP:(c + 1) * NSAMP],
            "wt": wt, "wpool": wpool, "colvec": colvec, "bv": bv, "sinks": sinks,
            "c_ident": ident, "c_perm": perm, "c_masks": masks, "c_rope": rope,
        })
    nc = _program(_n_reg_tiles, _stage)
    in_maps = in_maps[:_ncores]
    res = run_bass_kernel_spmd(nc, in_maps, core_ids=list(range(_ncores)))
    R = res.results
    if _dbg is not None:
        _dbg.extend([r.get('dbg') for r in R])
    cat = lambda k: np.concatenate([r[k] for r in R] + [np.zeros_like(R[0][k])] * (NCORE - _ncores), 0)
    y_prompt = cat("yp").reshape(16, SEQ, D)
    y_sample = cat("ys").reshape(128, 1, D)
    nkp = cat("nkp").reshape(1, 16, 128, 4, 64); nvp = cat("nvp").reshape(1, 16, 128, 4, 64)
    npp = cat("npp").reshape(1, 16, 15, 1024)
    nks = cat("nks").reshape(1, 128, 128, 4, 64); nvs = cat("nvs").reshape(1, 128, 128, 4, 64)
    nps = cat("nps").reshape(1, 128, 15, 1024)
    return (y_prompt, y_sample, nkp, nvp, npp, nks, nvs, nps)
```
